# Optimizing a Trainium2 kernel written in Bass

```python
import jax, jax.numpy as jnp
from jax import lax
import numpy as np

D_MODEL = 1024
BATCH = 16
SEQ = 2048
DEPTH = 1
DEC_BATCH = 16
DEC_SEQ = 64
PAST_LEN = 4096

CHUNK = 64
D_MIX = D_MODEL
D_DELTA = D_MIX // 2
D_POOL = D_MIX - D_DELTA
N_DHEADS = 4
HEAD_DIM = D_DELTA // N_DHEADS
D_QKV = 3 * D_DELTA
CONV_W = 4
POOL_WINDOWS = (2, 4, 8, 16)
N_POOL_GROUPS = len(POOL_WINDOWS)
POOL_GROUP = D_POOL // N_POOL_GROUPS
POOL_HIST = max(POOL_WINDOWS) - 1
D_IN = D_QKV + D_DELTA + 2 * N_DHEADS + D_POOL
D_FF = -(-8 * D_MODEL // (3 * 256)) * 256
EPS = 1e-6

kernel_name = "hybrid_gdn_pool_stream_step"


def rmsnorm(x, w):
    xf = x.astype(jnp.float32)
    y = xf * lax.rsqrt(jnp.mean(xf * xf, axis=-1, keepdims=True) + EPS)
    return (y * w.astype(jnp.float32)).astype(x.dtype)


def l2norm(x):
    return x * lax.rsqrt(jnp.sum(x * x, axis=-1, keepdims=True) + EPS)


def causal_short_conv(u, prev, w):
    L = u.shape[1]
    ext = jnp.concatenate([prev.astype(u.dtype), u], axis=1)
    out = ext[:, 0:L] * w[0]
    for j in range(1, CONV_W):
        out = out + ext[:, j:j + L] * w[j]
    return jax.nn.silu(out), ext[:, -(CONV_W - 1):]


def gated_delta_chunked(q, k, v, g, beta, s0, chunk):
    B, H, L, dk = q.shape
    n = L // chunk
    r = lambda t: t.reshape(B, H, n, chunk, *t.shape[3:])
    q, k, v, g, beta = r(q), r(k), r(v), r(g), r(beta)
    G = jnp.cumsum(g, axis=-1)
    idx = jnp.arange(chunk)
    lower_incl = idx[:, None] >= idx[None, :]
    lower_strict = idx[:, None] > idx[None, :]
    gamma = jnp.exp(jnp.where(lower_incl, G[..., :, None] - G[..., None, :], -jnp.inf))
    kb = k * beta[..., None]
    Lm = jnp.where(lower_strict, jnp.einsum('bhncd,bhnsd->bhncs', kb, k) * gamma, 0.0)
    eye = jnp.eye(chunk, dtype=jnp.float32)
    T = lax.linalg.triangular_solve(eye + Lm, jnp.broadcast_to(eye, Lm.shape),
                                    left_side=True, lower=True)
    u = jnp.einsum('bhncs,bhnsv->bhncv', T, v * beta[..., None])
    w = jnp.einsum('bhncs,bhnsk->bhnck', T, kb * jnp.exp(G)[..., None])
    qk = jnp.einsum('bhncd,bhnsd->bhncs', q, k) * gamma
    q_dec = q * jnp.exp(G)[..., None]
    k_tail = k * jnp.exp(G[..., -1:] - G)[..., None]
    d_last = jnp.exp(G[..., -1])
    mv = lambda t: jnp.moveaxis(t, 2, 0)

    def step(S, xs):
        u_c, w_c, qk_c, qd_c, kt_c, dl_c = xs
        v_new = u_c - jnp.einsum('bhck,bhkv->bhcv', w_c, S)
        o_c = jnp.einsum('bhck,bhkv->bhcv', qd_c, S) + jnp.einsum('bhcs,bhsv->bhcv', qk_c, v_new)
        S = S * dl_c[..., None, None] + jnp.einsum('bhck,bhcv->bhkv', kt_c, v_new)
        return S, o_c

    S, o = lax.scan(step, s0, (mv(u), mv(w), mv(qk), mv(q_dec), mv(k_tail), mv(d_last)))
    o = jnp.moveaxis(o, 0, 2).reshape(B, H, L, v.shape[-1])
    return o, S


def multiscale_pool(u, prev, pos0, pool_w, pool_scale):
    B, L, _ = u.shape
    ext = jnp.concatenate([prev.astype(u.dtype), u], axis=1)
    extf = ext.astype(jnp.float32)
    cs = jnp.concatenate([jnp.zeros((B, 1, D_POOL), jnp.float32), jnp.cumsum(extf, axis=1)], axis=1)
    pos = pos0 + jnp.arange(L)
    outs = []
    for gi, win in enumerate(POOL_WINDOWS):
        sl = slice(gi * POOL_GROUP, (gi + 1) * POOL_GROUP)
        s = cs[:, POOL_HIST + 1:POOL_HIST + 1 + L, sl] - cs[:, POOL_HIST + 1 - win:POOL_HIST + 1 - win + L, sl]
        cnt = jnp.minimum(win, pos + 1).astype(jnp.float32)
        outs.append(s / cnt[None, :, None] - extf[:, POOL_HIST:, sl])
    d = jnp.stack(outs, axis=2)
    y = jnp.einsum('blgc,gcd->blgd', d, pool_w.astype(jnp.float32)).reshape(B, L, D_POOL)
    y = y * pool_scale.astype(jnp.float32)
    return y.astype(u.dtype), ext[:, -POOL_HIST:]


def hybrid_layer(x, conv_prev, s_prev, pool_prev, pos0, chunk, norm1_w, w_in, conv_w, a_log,
                 dt_bias, onorm_w, pool_w, pool_scale, w_out, norm2_w, w_gate, w_up, w_down):
    B, L, _ = x.shape
    h = rmsnorm(x, norm1_w)
    proj = h @ w_in
    o1 = D_QKV
    o2 = o1 + D_DELTA
    o3 = o2 + N_DHEADS
    o4 = o3 + N_DHEADS
    qkv_pre, z, a, b, u_pool = proj[..., :o1], proj[..., o1:o2], proj[..., o2:o3], proj[..., o3:o4], proj[..., o4:]
    qkv, conv_new = causal_short_conv(qkv_pre, conv_prev, conv_w)
    qkv = qkv.astype(jnp.float32).reshape(B, L, 3, N_DHEADS, HEAD_DIM).transpose(2, 0, 3, 1, 4)
    q = l2norm(qkv[0]) * (HEAD_DIM ** -0.5)
    k = l2norm(qkv[1])
    v = qkv[2]
    g = -jnp.exp(a_log.astype(jnp.float32)) * jax.nn.softplus(a.astype(jnp.float32) + dt_bias.astype(jnp.float32))
    beta = jax.nn.sigmoid(b.astype(jnp.float32))
    o, s_new = gated_delta_chunked(q, k, v, g.transpose(0, 2, 1), beta.transpose(0, 2, 1),
                                   s_prev.astype(jnp.float32), chunk)
    o = o.transpose(0, 2, 1, 3)
    o = rmsnorm(o, onorm_w) * jax.nn.silu(z.astype(jnp.float32).reshape(B, L, N_DHEADS, HEAD_DIM))
    o_delta = o.reshape(B, L, D_DELTA).astype(x.dtype)
    o_pool, pool_new = multiscale_pool(u_pool, pool_prev, pos0, pool_w, pool_scale)
    x = x + jnp.concatenate([o_delta, o_pool], axis=-1) @ w_out
    h = rmsnorm(x, norm2_w)
    x = x + (jax.nn.silu(h @ w_gate) * (h @ w_up)) @ w_down
    return x, conv_new, s_new.astype(x.dtype), pool_new


def setup_inputs(seed: int = 0) -> dict:
    key = jax.random.key(seed)
    ks = jax.random.split(key, 24)
    nrm = lambda k, shape, s: jax.random.normal(k, shape, jnp.float32) * s
    dt = jnp.exp(jax.random.uniform(ks[8], (DEPTH, N_DHEADS), jnp.float32, np.log(1e-3), np.log(0.1)))
    return {
        "x_prompt": nrm(ks[0], (BATCH, SEQ, D_MODEL), 1.0),
        "x_sample": nrm(ks[1], (DEC_BATCH, DEC_SEQ, D_MODEL), 1.0),
        "state_conv": nrm(ks[2], (DEPTH, DEC_BATCH, CONV_W - 1, D_QKV), 1.0),
        "state_delta": nrm(ks[3], (DEPTH, DEC_BATCH, N_DHEADS, HEAD_DIM, HEAD_DIM), 0.3),
        "state_pool": nrm(ks[4], (DEPTH, DEC_BATCH, POOL_HIST, D_POOL), 1.0),
        "norm1_w": 1.0 + nrm(ks[5], (DEPTH, D_MODEL), 0.05),
        "w_in": nrm(ks[6], (DEPTH, D_MODEL, D_IN), D_MODEL ** -0.5),
        "conv_w": nrm(ks[7], (DEPTH, CONV_W, D_QKV), CONV_W ** -0.5),
        "a_log": jnp.log(jax.random.uniform(ks[9], (DEPTH, N_DHEADS), jnp.float32, 1.0, 16.0)),
        "dt_bias": dt + jnp.log(-jnp.expm1(-dt)),
        "onorm_w": 1.0 + nrm(ks[10], (DEPTH, HEAD_DIM), 0.05),
        "pool_w": nrm(ks[11], (DEPTH, N_POOL_GROUPS, POOL_GROUP, POOL_GROUP), POOL_GROUP ** -0.5),
        "pool_scale": 1.0 + nrm(ks[12], (DEPTH, D_POOL), 0.1),
        "w_out": nrm(ks[13], (DEPTH, D_MIX, D_MODEL), D_MIX ** -0.5),
        "norm2_w": 1.0 + nrm(ks[14], (DEPTH, D_MODEL), 0.05),
        "w_gate": nrm(ks[15], (DEPTH, D_MODEL, D_FF), D_MODEL ** -0.5),
        "w_up": nrm(ks[16], (DEPTH, D_MODEL, D_FF), D_MODEL ** -0.5),
        "w_down": nrm(ks[17], (DEPTH, D_FF, D_MODEL), D_FF ** -0.5),
        "normf_w": 1.0 + nrm(ks[18], (D_MODEL,), 0.05),
    }


def reference(x_prompt, x_sample, state_conv, state_delta, state_pool, norm1_w, w_in, conv_w, a_log,
              dt_bias, onorm_w, pool_w, pool_scale, w_out, norm2_w, w_gate, w_up, w_down, normf_w):
    bp = x_prompt.shape[0]
    dec_len = x_sample.shape[1]
    hp, hs = x_prompt, x_sample
    cp, sp, pp, cs, ss, ps = [], [], [], [], [], []
    for l in range(DEPTH):
        lw = (norm1_w[l], w_in[l], conv_w[l], a_log[l], dt_bias[l], onorm_w[l], pool_w[l],
              pool_scale[l], w_out[l], norm2_w[l], w_gate[l], w_up[l], w_down[l])
        zc = jnp.zeros((bp, CONV_W - 1, D_QKV), x_prompt.dtype)
        zs = jnp.zeros((bp, N_DHEADS, HEAD_DIM, HEAD_DIM), jnp.float32)
        zp = jnp.zeros((bp, POOL_HIST, D_POOL), x_prompt.dtype)
        hp, c1, s1, p1 = hybrid_layer(hp, zc, zs, zp, 0, CHUNK, *lw)
        hs, c2, s2, p2 = hybrid_layer(hs, state_conv[l], state_delta[l], state_pool[l], PAST_LEN, dec_len, *lw)
        cp.append(c1); sp.append(s1); pp.append(p1)
        cs.append(c2); ss.append(s2); ps.append(p2)
    y_prompt = rmsnorm(hp, normf_w)
    y_sample = rmsnorm(hs, normf_w)
    return (y_prompt, y_sample, jnp.stack(cp), jnp.stack(sp), jnp.stack(pp),
            jnp.stack(cs), jnp.stack(ss), jnp.stack(ps))
```

```python
import contextlib
import numpy as np
import concourse.bass as bass
import concourse.mybir as mybir
from concourse.bass_utils import run_bass_kernel_spmd

F32 = mybir.dt.float32
BF16 = mybir.dt.bfloat16
AF = mybir.ActivationFunctionType
ALU = mybir.AluOpType

ENGS = ("pe", "act", "dve", "pool", "sp")
D = 1024
DFF = 2816
NFF = 22
EPS = 1e-6


class Buf:
    __slots__ = ("name", "last_write", "reads", "dsem", "dcount", "excl")

    def __init__(self, name=""):
        self.name = name
        self.excl = False
        self.last_write = None
        self.reads = []
        self.dsem = None
        self.dcount = 0


class Fw:
    def __init__(self, nc, stack):
        self.nc = nc
        self.ops = {e: [] for e in ENGS}
        self.cnt = {e: 0 for e in ENGS}
        self.sems = {}
        self.waited = {}
        self._stack = stack
        self.n_dsem = 0
        self.dma_points = {}
        self.ninst = {e: 0 for e in ENGS}
        self.dead = False
        self.maxops = None
        self.nops = 0

    def _sem(self, key):
        if key not in self.sems:
            self.sems[key] = self._stack.enter_context(self.nc.semaphore("s_%s" % (str(key).replace(" ", ""))))
        return self.sems[key]

    def _need(self, eng, deps):
        for key, val in deps.items():
            if eng == "pe" and key == "pe":
                continue
            if self.waited.get((eng, key), 0) >= val:
                continue
            self.waited[(eng, key)] = val
            sem = self._sem(key)
            self.ops[eng].append(lambda e, sem=sem, val=val: e.wait_ge(sem, val))
            self.ninst[eng] += 1

    @staticmethod
    def _add(deps, sp):
        if sp is None:
            return
        k, v = sp
        if deps.get(k, 0) < v:
            deps[k] = v

    def _deps(self, reads, writes):
        deps = {}
        for b in reads:
            self._add(deps, b.last_write)
        for b in writes:
            self._add(deps, b.last_write)
            for r in b.reads:
                self._add(deps, r)
        return deps

    def op(self, eng, fn, reads=(), writes=(), inc=True):
        self.nops += 1
        if self.dead or (self.maxops is not None and self.nops > self.maxops):
            return None
        ex = [b for b in reads if b.excl]
        if ex:
            reads = [b for b in reads if not b.excl]
            writes = list(writes) + [b for b in ex if b not in writes]
        self._need(eng, self._deps(reads, writes))
        idx = self.cnt[eng] + 1
        sp = (eng, idx)
        if inc:
            sem = self._sem(eng)
            self.ops[eng].append(lambda e, fn=fn, sem=sem: fn(e).then_inc(sem, 1))
            self.cnt[eng] = idx
        else:
            self.ops[eng].append(lambda e, fn=fn: fn(e))
        self.ninst[eng] += 1
        for b in reads:
            b.reads.append(sp)
        for b in writes:
            b.last_write = sp
            b.reads = []
        return sp

    def dma(self, q, out_ap, in_ap, sbuf_buf, reads=(), writes=(), **kw):
        self.nops += 1
        if self.dead or (self.maxops is not None and self.nops > self.maxops):
            return None
        self._need(q, self._deps(reads, writes))
        b = sbuf_buf
        if b.dsem is None:
            b.dsem = {}
        if q not in b.dsem:
            b.dsem[q] = [("d", self.n_dsem), 0]
            self.n_dsem += 1
        ent = b.dsem[q]
        sem = self._sem(ent[0])
        ent[1] += 16
        sp = (ent[0], ent[1])
        self.ops[q].append(
            lambda e, o=out_ap, i=in_ap, sem=sem, kw=kw: e.dma_start(out=o, in_=i, **kw).then_inc(sem, 16))
        self.ninst[q] += 1
        for r in reads:
            r.reads.append(sp)
        for w in writes:
            w.last_write = sp
            w.reads = []
        self.dma_points[ent[0]] = ent[1]
        return sp

    def barrier(self):
        if self.dead:
            return
        deps = {e: self.cnt[e] for e in ENGS if self.cnt[e] > 0}
        deps.update(self.dma_points)
        for e in ENGS:
            d = dict(deps)
            d.pop(e, None) if e == "pe" else None
            self._need(e, d)

    def finish(self):
        self._need("sp", dict(self.dma_points))

    def emit(self):
        ops = self.ops
        with self.nc.Block() as block:
            @block.tensor
            def _(e):
                for f in ops["pe"]:
                    f(e)

            @block.scalar
            def _(e):
                for f in ops["act"]:
                    f(e)

            @block.vector
            def _(e):
                for f in ops["dve"]:
                    f(e)

            @block.gpsimd
            def _(e):
                for f in ops["pool"]:
                    f(e)

            @block.sync
            def _(e):
                for f in ops["sp"]:
                    f(e)


class T:
    def __init__(self, t, name):
        self.t = t
        self.b = Buf(name)

    def __getitem__(self, k):
        return self.t[k]


class _Stop(Exception):
    pass


def build_program(NP, LP, NS, LS=64, NT1=256, NT2=256, debug=False, stage=99, maxops=None, PST=1, UST=1, UK=None, PK=8, ABPF=0):
    nc = bass.Bass("TRN2", target_bir_lowering=False)
    ntok = NP * LP + NS * LS

    def din(name, shape):
        return nc.dram_tensor(name, list(shape), F32, kind="ExternalInput").ap()

    def dout(name, shape):
        return nc.dram_tensor(name, list(shape), F32, kind="ExternalOutput").ap()

    xp = din("xp", [NP, LP, D])
    xs = din("xs", [NS, LS, D])
    sconv = din("sconv", [NS, 3, 1536])
    sdelta = din("sdelta", [NS, 4, 128, 128])
    spool = din("spool", [NS, 15, 512])
    norm1_w = din("norm1_w", [D])
    w_in = din("w_in", [D, 2568])
    conv_w = din("conv_w", [4, 1536])
    a_log = din("a_log", [4])
    dt_bias = din("dt_bias", [4])
    onorm_w = din("onorm_w", [128])
    pool_w = din("pool_w", [4, 128, 128])
    pool_scale = din("pool_scale", [512])
    w_out = din("w_out", [D, D])
    norm2_w = din("norm2_w", [D])
    w_gate = din("w_gate", [D, DFF])
    w_up = din("w_up", [D, DFF])
    w_down = din("w_down", [DFF, D])
    normf_w = din("normf_w", [D])
    c_ident = din("c_ident", [128, 128])
    c_U = din("c_U", [128, 128])
    c_SU = din("c_SU", [128, 128])
    c_invcnt = din("c_invcnt", [128, 4, 16])
    c_bmask = din("c_bmask", [4, 128, 128])

    yp = dout("yp", [NP, LP, D])
    ys = dout("ys", [NS, LS, D])
    ncp = dout("ncp", [NP, 3, 1536])
    ndp = dout("ndp", [NP, 4, 128, 128])
    npp = dout("npp", [NP, 15, 512])
    ncs = dout("ncs", [NS, 3, 1536])
    nds = dout("nds", [NS, 4, 128, 128])
    nps = dout("nps", [NS, 15, 512])

    x1s = nc.dram_tensor("x1s", [ntok, D], F32, kind=("ExternalOutput" if debug else "Internal")).ap()
    x1s_b = Buf("x1s")
    if debug:
        dbg1 = dout("dbg1", [ntok, D])
        dbg2 = dout("dbg2", [ntok, D])
        dbg_h = dout("dbg_h", [128, 8, NT2])
        dbg_a = dout("dbg_a", [128, NFF, NT2])

    with contextlib.ExitStack() as st:
      try:
        fw = Fw(nc, st)
        fw.maxops = maxops

        def stop_if(k):
            if stage <= k:
                if not fw.dead:
                    print('STOP stage', k, 'nops', fw.nops)
                fw.dead = True

        def mk(stack, name, shape, dt=F32):
            return T(stack.enter_context(nc.sbuf_tensor(name, list(shape), dt)), name)

        def mkp(stack, name, shape, dt=F32):
            t_ = T(stack.enter_context(nc.psum_tensor(name, list(shape), dt)), name)
            t_.b.excl = True
            return t_

        ident = mk(st, "ident", [128, 128])
        identb = mk(st, "identb", [128, 128], BF16)
        U32 = mk(st, "U32", [128, 128])
        mUI = mk(st, "mUI", [128, 128])
        mSU = mk(st, "mSU", [128, 128])
        ones32 = mk(st, "ones32", [128, 128])
        onesq = mk(st, "onesq", [128, 128], BF16)
        onesm = mk(st, "onesm", [128, 128], BF16)
        invcnt = mk(st, "invcnt", [128, 4, 16])
        cw = mk(st, "cw", [128, 12, 4])
        n1w = mk(st, "n1w", [128, 8])
        n2w = mk(st, "n2w", [128, 8])
        onw = mk(st, "onw", [128, 1])
        psc = mk(st, "psc", [128, 4])
        nfw = mk(st, "nfw", [128, D])
        dtb = mk(st, "dtb", [128, 4])
        negA = mk(st, "negA", [128, 4])
        ceps = mk(st, "ceps", [128, 4])
        rows = mk(st, "rows", [16, 512])

        PB = [mkp(st, "pb%d" % i, [128, 512]) for i in range(6)]
        PT = [mkp(st, "pt%d" % i, [128, 1024], BF16) for i in range(2)]
        pstate = {"f": 0, "b": 0}

        import collections
        free = {"f": collections.deque(PB), "b": collections.deque(PT), "raw": collections.deque([0, 1, 2, 3, 4, 5, 6, 7]),
                "pool": collections.deque([0]), "l2": collections.deque([0, 1, 2, 3]), "on": collections.deque([0, 1, 2, 3]),
                "prep": collections.deque([0, 1, 2, 3]), "xt": collections.deque([0, 1])}

        def pbank():
            p = free["f"].popleft()
            free["f"].append(p)
            return p

        def ptbank():
            p = free["b"].popleft()
            free["b"].append(p)
            return p

        def acquire(kind):
            while not free[kind]:
                yield
            return free[kind].popleft()

        def release(kind, p):
            free[kind].append(p)

        def act(fn, reads, writes):
            return fw.op("act", fn, [r.b for r in reads], [w.b for w in writes])

        def dve(fn, reads, writes):
            return fw.op("dve", fn, [r.b for r in reads], [w.b for w in writes])

        def gps(fn, reads, writes):
            return fw.op("pool", fn, [r.b for r in reads], [w.b for w in writes])

        def pe(fn, reads, writes, inc=True):
            return fw.op("pe", fn, [r.b for r in reads], [w.b for w in writes], inc=inc)

        def load(q, dst, dst_ap, src_ap, extra_reads=(), **kw):
            return fw.dma(q, dst_ap, src_ap, dst.b, reads=list(extra_reads), writes=[dst.b], **kw)

        load("sp", ident, ident[:], c_ident)
        load("sp", U32, U32[:], c_U)
        load("sp", mSU, mSU[:], c_SU)
        load("sp", invcnt, invcnt[:], c_invcnt)
        load("sp", n1w, n1w[:], norm1_w.rearrange("(k p) -> p k", p=128), allow_slow_non_contiguous=True)
        load("sp", n2w, n2w[:], norm2_w.rearrange("(k p) -> p k", p=128), allow_slow_non_contiguous=True)
        load("sp", onw, onw[:], onorm_w.rearrange("(p o) -> p o", o=1), allow_slow_non_contiguous=True)
        load("sp", psc, psc[:], pool_scale.rearrange("(g p) -> p g", p=128), allow_slow_non_contiguous=True)
        load("sp", nfw, nfw[:], normf_w.partition_broadcast(128))
        load("sp", dtb, dtb[:], dt_bias.partition_broadcast(128))
        load("sp", negA, negA[:], a_log.partition_broadcast(128))
        bm2 = [mk(st, "bm2_%d" % l, [128, 2, 128], BF16) for l in range(4)]
        II = mk(st, "II", [128, 2, 128], BF16)
        for l in range(4):
            load("sp", ones32, ones32[:], c_bmask[l])
            dve(lambda e, l=l: e.tensor_copy(bm2[l][:, 0, :], ones32[:]), [ones32], [bm2[l]])
            dve(lambda e, l=l: e.tensor_copy(bm2[l][:, 1, :], ones32[:]), [ones32], [bm2[l]])
        dve(lambda e: e.tensor_copy(identb[:], ident[:]), [ident], [identb])
        dve(lambda e: e.tensor_copy(II[:, 0, :], ident[:]), [ident], [II])
        dve(lambda e: e.tensor_copy(II[:, 1, :], ident[:]), [ident], [II])
        dve(lambda e: e.tensor_add(mUI[:], mSU[:], ident[:]), [mSU, ident], [mUI])
        dve(lambda e: e.memset(ones32[:], 1.0), [], [ones32])
        dve(lambda e: e.memset(onesq[:], 1.0), [], [onesq])
        dve(lambda e: e.memset(onesm[:], 1.0 / 128.0), [], [onesm])
        dve(lambda e: e.memset(ceps[:, 0:1], EPS), [], [ceps])
        dve(lambda e: e.memset(ceps[:, 1:2], 128.0 * EPS), [], [ceps])
        dve(lambda e: e.memset(ceps[:, 2:3], 1.0), [], [ceps])
        act(lambda e: e.activation(negA[:], negA[:], AF.Exp), [negA], [negA])
        dve(lambda e: e.tensor_scalar(negA[:], negA[:], -1.0, None, ALU.mult), [negA], [negA])

        def rows_to_fm(src_ap, nrows, nch, dst, dst_fn):
            for g0 in range(0, nch, 4):
                load("sp", rows, rows[0:nrows, 0:512], src_ap[:, g0 * 128:(g0 + 4) * 128])
                pb = pbank()
                for c4 in range(4):
                    pe(lambda e, c4=c4, pb=pb: e.transpose(pb[:, c4 * 16:c4 * 16 + nrows], rows[0:nrows, c4 * 128:(c4 + 1) * 128],
                                                           ident[0:nrows, 0:nrows]), [rows, ident], [pb])
                for c4 in range(4):
                    dve(lambda e, c4=c4, pb=pb, g0=g0: e.tensor_copy(dst_fn(g0 + c4), pb[:, c4 * 16:c4 * 16 + nrows]), [pb], [dst])

        rows_to_fm(conv_w, 4, 12, cw, lambda ch: cw[:, ch, :])
        stop_if(1)

        with contextlib.ExitStack() as sa:
            Win = mk(sa, "Win", [128, 8, 2560], BF16)
            Wab = mk(sa, "Wab", [128, 8, 8], BF16)
            Wout = mk(sa, "Wout", [128, 8, D], BF16)
            poolw = mk(sa, "poolw", [128, 4, 128], BF16)
            with contextlib.ExitStack() as s0:
                stg = [mk(s0, "stgA%d" % i, [128, 2568]) for i in range(2)]
                for kc in range(8):
                    s = stg[kc % 2]
                    load("sp", s, s[:], w_in[kc * 128:(kc + 1) * 128, :])
                    act(lambda e, s=s, kc=kc: e.activation(Win[:, kc, 0:1024], s[:, 0:1024], AF.Copy,
                                                           scale=n1w[:, kc:kc + 1]), [s, n1w], [Win])
                    dve(lambda e, s=s, kc=kc: e.tensor_scalar(Win[:, kc, 1024:2048], s[:, 1024:2048],
                                                              n1w[:, kc:kc + 1], None, ALU.mult), [s, n1w], [Win])
                    dve(lambda e, s=s, kc=kc: e.tensor_scalar(Win[:, kc, 2048:2560], s[:, 2056:2568],
                                                              n1w[:, kc:kc + 1], None, ALU.mult), [s, n1w], [Win])
                    act(lambda e, s=s, kc=kc: e.activation(Wab[:, kc, :], s[:, 2048:2056], AF.Copy,
                                                           scale=n1w[:, kc:kc + 1]), [s, n1w], [Wab])
                for kc in range(8):
                    s = stg[kc % 2]
                    load("sp", s, s[:, 0:D], w_out[kc * 128:(kc + 1) * 128, :])
                    sc = onw[:, 0:1] if kc < 4 else psc[:, kc - 4:kc - 3]
                    scb = onw if kc < 4 else psc
                    if kc % 2 == 0:
                        act(lambda e, s=s, kc=kc, sc=sc: e.activation(Wout[:, kc, :], s[:, 0:D], AF.Copy, scale=sc),
                            [s, scb], [Wout])
                    else:
                        dve(lambda e, s=s, kc=kc, sc=sc: e.tensor_scalar(Wout[:, kc, :], s[:, 0:D], sc, None, ALU.mult),
                            [s, scb], [Wout])
                s = stg[0]
                load("sp", s, s[:, 0:512].rearrange("p (g d) -> p g d", g=4), pool_w.rearrange("g c d -> c g d"))
                dve(lambda e, s=s: e.tensor_copy(poolw[:].rearrange("p g d -> p (g d)"), s[:, 0:512]), [s], [poolw])
                fw.barrier()
                stop_if(2)

            NT = NT1
            NCH = NT // 128
            NU = NCH * 4
            xt = [mk(sa, "xt%d" % i, [128, D]) for i in range(2)]
            hb = [mk(sa, "hb%d" % i, [128, D], BF16) for i in range(2)]
            hT = mk(sa, "hT", [128, 8, NT], BF16)
            raw = [mk(sa, "raw%d" % i, [128, 3 + NT]) for i in range(8)]
            rawh = [T(None, "rawh") for i in range(8)]
            acc = [mk(sa, "acc%d" % i, [128, NT]) for i in range(8)]
            qk32 = mk(sa, "qk32", [128, 8, NT])
            sqb = [mk(sa, "sqb%d" % i, [128, NT], BF16) for i in range(4)]
            rn = [mk(sa, "rn%d" % i, [128, NT]) for i in range(4)]
            qnT = mk(sa, "qnT", [128, 4, NT], BF16)
            knT = mk(sa, "knT", [128, 4, NT], BF16)
            vT = mk(sa, "vT", [128, 4, NT], BF16)
            zs = mk(sa, "zs", [128, 4, NT], BF16)
            pext = [mk(sa, "pext%d" % i, [128, 15 + NT]) for i in range(2)]
            ps2 = mk(sa, "ps2", [128, 15 + NT])
            ps4 = mk(sa, "ps4", [128, 15 + NT])
            ps8 = mk(sa, "ps8", [128, 15 + NT])
            ps16 = mk(sa, "ps16", [128, 15 + NT])
            dT = mk(sa, "dT", [128, 4, NT], BF16)
            hist = mk(sa, "hist", [128, 12, 3])
            phist = mk(sa, "phist", [128, 4, 15])
            S32 = mk(sa, "S32", [128, 4, 128])
            Sbf = mk(sa, "Sbf", [128, 4, 128], BF16)
            NS16 = NCH * 4
            sm = {k: mk(sa, "sm_" + k, [128, NS16]) for k in ["ssq", "r1", "rstd"]}
            smp = [{k: mk(sa, "sm%d_" % p_ + k, [128, NS16]) for k in
                    ["ap", "e1", "sp", "g", "e2", "beta", "nbeta", "G", "eG", "tk", "ekt", "dl", "nbe"]} for p_ in range(2)]
            rms_done = {}
            vb = mk(sa, "vb", [128, NCH, 4, 128])
            ktl = mk(sa, "ktl", [128, NCH, 4, 128], BF16)
            gbc = [mk(sa, "gbc%d" % i, [128, 128]) for i in range(4)]
            Dm = [mk(sa, "Dm%d" % i, [128, 128]) for i in range(4)]
            Gam = [mk(sa, "Gam%d" % i, [128, 128]) for i in range(4)]
            GM = [mk(sa, "GM%d" % i, [128, 128]) for i in range(4)]
            GMs = [mk(sa, "GMs%d" % i, [128, 128]) for i in range(4)]
            eGr = [mk(sa, "eGr%d" % i, [128, 128]) for i in range(4)]
            UT = [mk(sa, "UT%d" % u, [128, 4, 128], BF16) for u in range(NU)]
            AB0 = [mk(sa, "AB0%d" % u, [128, 2, 128], BF16) for u in range(NU)]
            UTA = [T(None, "uta") for u in range(NU)]
            UTY = [T(None, "uty") for u in range(NU)]
            MO = [mk(sa, "MO%d" % u, [128, 2, 128], BF16) for u in range(NU)]
            PQ = [mk(sa, "PQ%d" % u, [128, 2, 128], BF16) for u in range(NU)]
            Yf = [mk(sa, "Yf%d" % u, [128, 128], BF16) for u in range(NU)]
            QKm = [mk(sa, "QKm%d" % u, [128, 128], BF16) for u in range(NU)]
            qdT = mk(sa, "qdT", [128, 4, NT], BF16)
            Br = [mk(sa, "Br%d" % i, [128, 128], BF16) for i in range(4)]
            vn = [mk(sa, "vn%d" % i, [128, 128], BF16) for i in range(4)]
            oT32 = mk(sa, "oT32", [128, 4, NT])
            SbfH = [T(None, "sbfh") for h in range(4)]
            qk32S = [T(None, "qk32s") for _ in range(8)]
            vTS = [T(None, "vts") for _ in range(4)]
            zsS = [T(None, "zss") for _ in range(4)]
            qnTS = [T(None, "qnts") for _ in range(4)]
            knTS = [T(None, "knts") for _ in range(4)]
            qdTS = [T(None, "qdts") for _ in range(NU)]
            S32H = [T(None, "s32h") for h in range(4)]
            oTH = [T(None, "oth") for h in range(4)]
            osq = [mk(sa, "osq%d" % i, [128, NT], BF16) for i in range(4)]
            orn = [mk(sa, "orn%d" % i, [128, NT]) for i in range(4)]
            mixT = mk(sa, "mixT", [128, 8, NT], BF16)
            x1t = [mk(sa, "x1t%d" % i, [128, D]) for i in range(2)]
            orow = mk(sa, "orow", [16, 512])
            ctr = {"xt": 0, "tmp": 0, "x1t": 0}


            def rmsnorm_T(src_rows_ap, C, xtile, hbt, hT_t, col0, s_ssq, s_r1, s_rstd, sidx):
                load("sp", xtile, xtile[0:C, :], src_rows_ap)
                act(lambda e: e.activation(hbt[0:C, :], xtile[0:C, :], AF.Square,
                                           accum_out=s_ssq[0:C, sidx:sidx + 1]), [xtile], [hbt, s_ssq])
                act(lambda e: e.activation(s_r1[0:C, sidx:sidx + 1], s_ssq[0:C, sidx:sidx + 1], AF.Ln,
                                           bias=ceps[0:C, 0:1], scale=1.0 / D), [s_ssq, ceps], [s_r1])
                act(lambda e: e.activation(s_rstd[0:C, sidx:sidx + 1], s_r1[0:C, sidx:sidx + 1], AF.Exp, scale=-0.5),
                    [s_r1], [s_rstd])
                dve(lambda e: e.tensor_scalar(hbt[0:C, :], xtile[0:C, :], s_rstd[0:C, sidx:sidx + 1], None, ALU.mult),
                    [xtile, s_rstd], [hbt])
                pt = ptbank()
                for k in range(8):
                    pe(lambda e, k=k: e.transpose(pt[:, k * 128:k * 128 + C], hbt[0:C, k * 128:(k + 1) * 128],
                                                  identb[0:C, 0:C]), [hbt, identb], [pt])
                act(lambda e: e.copy(hT_t[:, :, col0:col0 + C],
                                     pt[:].rearrange("p (k c) -> p k c", k=8)[:, :, 0:C]), [pt], [hT_t])

            def run_tasks(gens, k=None, stagger=0):
                pending = list(gens)
                active = []
                rnd = 0
                last_admit = -10 ** 9
                while pending or active:
                    while pending and (k is None or len(active) < k) and (rnd - last_admit >= stagger or not active):
                        active.append(pending.pop(0))
                        last_admit = rnd
                        if stagger:
                            break
                    rnd += 1
                    for g_ in list(active):
                        try:
                            next(g_)
                        except StopIteration:
                            active.remove(g_)

            def rms_chain(xrows_fn, C, j, key=None):
                if True:
                    i = yield from acquire("xt")
                    xtile, hbt = xt[i], hb[i]
                    s_ssq, s_r1, s_rstd = sm["ssq"], sm["r1"], sm["rstd"]
                    load("sp", xtile, xtile[0:C, :], xrows_fn(j * C, C))
                    act(lambda e: e.activation(hbt[0:C, :], xtile[0:C, :], AF.Square,
                                               accum_out=s_ssq[0:C, j:j + 1]), [xtile], [hbt, s_ssq])
                    yield
                    act(lambda e: e.activation(s_r1[0:C, j:j + 1], s_ssq[0:C, j:j + 1], AF.Ln,
                                               bias=ceps[0:C, 0:1], scale=1.0 / D), [s_ssq, ceps], [s_r1])
                    yield
                    act(lambda e: e.activation(s_rstd[0:C, j:j + 1], s_r1[0:C, j:j + 1], AF.Exp, scale=-0.5),
                        [s_r1], [s_rstd])
                    yield
                    dve(lambda e: e.tensor_scalar(hbt[0:C, :], xtile[0:C, :], s_rstd[0:C, j:j + 1], None, ALU.mult),
                        [xtile, s_rstd], [hbt])
                    yield
                    pt = yield from acquire("b")
                    for k in range(8):
                        pe(lambda e, k=k: e.transpose(pt[:, k * 128:k * 128 + C], hbt[0:C, k * 128:(k + 1) * 128],
                                                      identb[0:C, 0:C]), [hbt, identb], [pt])
                    yield
                    act(lambda e: e.copy(hT[:, :, j * C:(j + 1) * C],
                                         pt[:].rearrange("p (k c) -> p k c", k=8)[:, :, 0:C]), [pt], [hT])
                    release("b", pt)
                    release("xt", i)
                    rms_done[key] = rms_done.get(key, 0) + 1
                    yield

            def ab_chain(C, nch, sm, key=None):
                n16 = nch * 4
                while key is not None and rms_done.get(key, 0) < nch:
                    yield
                pab = yield from acquire("f")
                for j in range(nch):
                    for kc in range(8):
                        pe(lambda e, j=j, kc=kc: e.matmul(pab[0:C, j * 8:(j + 1) * 8], hT[:, kc, j * C:(j + 1) * C],
                                                          Wab[:, kc, :], start=(kc == 0), stop=(kc == 7)),
                           [hT, Wab], [pab], inc=(kc == 7))
                yield
                pab3 = pab[0:C, 0:nch * 8].rearrange("p (j e) -> p j e", e=8)

                def v3(t):
                    return t[0:C, 0:n16].rearrange("p (j h) -> p j h", h=4)
                for j in range(nch):
                    dve(lambda e, j=j: e.tensor_add(sm["ap"][0:C, j * 4:(j + 1) * 4], pab[0:C, j * 8:j * 8 + 4],
                                                    dtb[0:C, :]), [pab, dtb], [sm["ap"]])
                yield
                act(lambda e: e.activation(v3(sm["e2"]), pab3[:, :, 4:8], AF.Exp, scale=-1.0), [pab], [sm["e2"]])
                release("f", pab)
                act(lambda e: e.activation(sm["e1"][0:C, 0:n16], sm["ap"][0:C, 0:n16], AF.Exp), [sm["ap"]], [sm["e1"]])
                yield
                act(lambda e: e.activation(sm["sp"][0:C, 0:n16], sm["e1"][0:C, 0:n16], AF.Ln, bias=ceps[0:C, 2:3]),
                    [sm["e1"], ceps], [sm["sp"]])
                dve(lambda e: e.tensor_scalar(sm["e2"][0:C, 0:n16], sm["e2"][0:C, 0:n16], 1.0, None, ALU.add),
                    [sm["e2"]], [sm["e2"]])
                yield
                for j in range(nch):
                    dve(lambda e, j=j: e.tensor_mul(sm["g"][0:C, j * 4:(j + 1) * 4], sm["sp"][0:C, j * 4:(j + 1) * 4],
                                                    negA[0:C, :]), [sm["sp"], negA], [sm["g"]])
                dve(lambda e: e.reciprocal(sm["beta"][0:C, 0:n16], sm["e2"][0:C, 0:n16]), [sm["e2"]], [sm["beta"]])
                yield
                dve(lambda e: e.tensor_scalar(sm["nbeta"][0:C, 0:n16], sm["beta"][0:C, 0:n16], -1.0, None, ALU.mult),
                    [sm["beta"]], [sm["nbeta"]])
                pg = yield from acquire("f")
                pe(lambda e: e.matmul(pg[0:C, 0:n16], U32[0:C, 0:C], sm["g"][0:C, 0:n16], start=True, stop=True),
                   [U32, sm["g"]], [pg], inc=False)
                pe(lambda e: e.matmul(pg[:, 64:64 + n16], ones32[0:C, :], sm["g"][0:C, 0:n16], start=True, stop=True),
                   [ones32, sm["g"]], [pg])
                yield
                dve(lambda e: e.tensor_copy(sm["G"][0:C, 0:n16], pg[0:C, 0:n16]), [pg], [sm["G"]])
                yield
                act(lambda e: e.activation(sm["eG"][0:C, 0:n16], pg[0:C, 0:n16], AF.Exp), [pg], [sm["eG"]])
                act(lambda e: e.activation(sm["dl"][:, 0:n16], pg[:, 64:64 + n16], AF.Exp), [pg], [sm["dl"]])
                yield
                dve(lambda e: e.tensor_sub(sm["tk"][0:C, 0:n16], pg[0:C, 64:64 + n16], sm["G"][0:C, 0:n16]),
                    [pg, sm["G"]], [sm["tk"]])
                release("f", pg)
                yield
                act(lambda e: e.activation(sm["ekt"][0:C, 0:n16], sm["tk"][0:C, 0:n16], AF.Exp), [sm["tk"]], [sm["ekt"]])
                dve(lambda e: e.tensor_scalar(sm["nbe"][0:C, 0:n16], sm["eG"][0:C, 0:n16], -1.0, None, ALU.mult),
                    [sm["eG"]], [sm["nbe"]])
                yield


            def smt(xrows_fn, x1row0, nt, C, first_of_prompt, do_rms, next_rms, sm, do_ab):
                nch = nt // C
                n16 = nch * 4
                nmerge = {128: 3, 64: 2}[C]
                if do_rms:
                    run_tasks([rms_chain(xrows_fn, C, j) for j in range(nch)])
                stop_if(3)

                def proj_chain(m):
                    pp = yield from acquire("f")
                    for kc in range(8):
                        pe(lambda e, kc=kc: e.matmul(pp[:, 0:nt], Win[:, kc, m * 128:(m + 1) * 128],
                                                     hT[:, kc, 0:nt], start=(kc == 0), stop=(kc == 7)),
                           [Win, hT], [pp], inc=(kc == 7))
                    yield
                    if m < 12:
                        i = yield from acquire("raw")
                        rw, ac, rwh = raw[i], acc[i], rawh[i]
                        gps(lambda e: e.tensor_copy(rw[:, 0:3], hist[:, m, :]), [hist], [rwh])
                        act(lambda e: e.copy(rw[:, 3:3 + nt], pp[:, 0:nt]), [pp], [rw])
                        yield
                        act(lambda e: e.activation(ac[:, 0:nt], pp[:, 0:nt], AF.Copy, scale=cw[:, m, 3:4]), [pp, cw], [ac])
                        release("f", pp)
                        yield
                        for jj in range(3):
                            dve(lambda e, jj=jj: e.scalar_tensor_tensor(
                                ac[:, 0:nt], rw[:, jj:jj + nt], cw[:, m, jj:jj + 1], ac[:, 0:nt], ALU.mult, ALU.add),
                                [rw, rwh, cw, ac], [ac])
                            yield
                        gps(lambda e: e.tensor_copy(hist[:, m, :], rw[:, nt:nt + 3]), [rw], [hist])
                        if m < 8:
                            act(lambda e: e.activation(qk32[:, m, 0:nt], ac[:, 0:nt], AF.Silu), [ac], [qk32S[m]])
                        else:
                            act(lambda e: e.activation(vT[:, m - 8, 0:nt], ac[:, 0:nt], AF.Silu), [ac], [vTS[m - 8]])
                        release("raw", i)
                        yield
                    elif m < 16:
                        act(lambda e: e.activation(zs[:, m - 12, 0:nt], pp[:, 0:nt], AF.Silu), [pp], [zsS[m - 12]])
                        release("f", pp)
                        yield
                    else:
                        g = m - 16
                        _slot = yield from acquire("pool")
                        px = pext[0]
                        L = 15 + nt
                        dve(lambda e: e.tensor_copy(px[:, 0:15], phist[:, g, :]), [phist], [px])
                        act(lambda e: e.copy(px[:, 15:15 + nt], pp[:, 0:nt]), [pp], [px])
                        release("f", pp)
                        yield
                        gps(lambda e: e.tensor_add(ps2[:, 1:L], px[:, 1:L], px[:, 0:L - 1]), [px], [ps2])
                        yield
                        cur = ps2
                        if g >= 1:
                            gps(lambda e: e.tensor_add(ps4[:, 3:L], ps2[:, 3:L], ps2[:, 1:L - 2]), [ps2], [ps4])
                            cur = ps4
                            yield
                        if g >= 2:
                            gps(lambda e: e.tensor_add(ps8[:, 7:L], ps4[:, 7:L], ps4[:, 3:L - 4]), [ps4], [ps8])
                            cur = ps8
                            yield
                        if g >= 3:
                            gps(lambda e: e.tensor_add(ps16[:, 15:L], ps8[:, 15:L], ps8[:, 7:L - 8]), [ps8], [ps16])
                            cur = ps16
                            yield
                        win = 2 ** (g + 1)
                        dve(lambda e: e.scalar_tensor_tensor(dT[:, g, 0:nt], cur[:, 15:L], 1.0 / win, px[:, 15:L],
                                                             ALU.mult, ALU.subtract), [cur, px], [dT])
                        if first_of_prompt:
                            dve(lambda e: e.tensor_mul(cur[:, 15:31], cur[:, 15:31], invcnt[:, g, :]), [cur, invcnt], [cur])
                            dve(lambda e: e.tensor_sub(dT[:, g, 0:16], cur[:, 15:31], px[:, 15:31]), [cur, px], [dT])
                        dve(lambda e: e.tensor_copy(phist[:, g, :], px[:, nt:nt + 15]), [px], [phist])
                        release("pool", 0)
                        yield
                        pq = yield from acquire("f")
                        pe(lambda e: e.matmul(pq[:, 0:nt], poolw[:, g, :], dT[:, g, 0:nt], start=True, stop=True),
                           [poolw, dT], [pq])
                        yield
                        act(lambda e: e.copy(mixT[:, 4 + g, 0:nt], pq[:, 0:nt]), [pq], [mixT])
                        release("f", pq)
                        yield

                run_tasks([proj_chain(m) for m in range(20)], k=PK, stagger=PST)

                def l2_chain(m):
                    i = yield from acquire("l2")
                    sq, rr = sqb[i], rn[i]
                    gps(lambda e: e.tensor_mul(sq[:, 0:nt], qk32[:, m, 0:nt], qk32[:, m, 0:nt]), [qk32S[m]], [sq])
                    yield
                    pn = yield from acquire("f")
                    pe(lambda e: e.matmul(pn[:, 0:nt], onesq[:], sq[:, 0:nt], start=True, stop=True), [onesq, sq], [pn])
                    yield
                    if m < 4:
                        act(lambda e: e.activation(rr[:, 0:nt], pn[:, 0:nt], AF.Ln, bias=ceps[:, 1:2], scale=128.0),
                            [pn, ceps], [rr])
                    else:
                        act(lambda e: e.activation(rr[:, 0:nt], pn[:, 0:nt], AF.Ln, bias=ceps[:, 0:1], scale=1.0),
                            [pn, ceps], [rr])
                    release("f", pn)
                    yield
                    act(lambda e: e.activation(rr[:, 0:nt], rr[:, 0:nt], AF.Exp, scale=-0.5), [rr], [rr])
                    yield
                    dstT = qnT if m < 4 else knT
                    dve(lambda e: e.tensor_mul(dstT[:, m % 4, 0:nt], qk32[:, m, 0:nt], rr[:, 0:nt]), [qk32S[m], rr], [(qnTS if m < 4 else knTS)[m % 4]])
                    release("l2", i)
                    yield

                run_tasks(([ab_chain(C, nch, sm)] if do_ab else []) + [l2_chain(m) for m in range(8)], k=5, stagger=1)
                stop_if(5)

                def tok_chain(j):
                    pt = yield from acquire("b")
                    for h in range(4):
                        pe(lambda e, h=h: e.transpose(pt[0:C, h * 128:(h + 1) * 128], vT[:, h, j * C:(j + 1) * C],
                                                      identb[:]), [*vTS, identb], [pt])
                    for h in range(4):
                        pe(lambda e, h=h: e.transpose(pt[0:C, 512 + h * 128:512 + (h + 1) * 128],
                                                      knT[:, h, j * C:(j + 1) * C], identb[:]), [*knTS, identb], [pt])
                    yield
                    dve(lambda e: e.tensor_copy(vb[0:C, j, :, :].rearrange("p h d -> p (h d)"), pt[0:C, 0:512]), [pt], [vb])
                    yield
                    for h in range(4):
                        idx = j * 4 + h
                        act(lambda e, h=h, idx=idx: e.activation(
                            ktl[0:C, j, h, :], pt[0:C, 512 + h * 128:512 + (h + 1) * 128], AF.Copy,
                            scale=sm["ekt"][0:C, idx:idx + 1]), [pt, sm["ekt"]], [ktl])
                    release("b", pt)
                    yield

                run_tasks([tok_chain(j) for j in range(nch)])
                stop_if(6)

                def unit_chain(j, h):
                    u = j * 4 + h
                    idx = u
                    cs = slice(j * C, (j + 1) * C)
                    i = yield from acquire("prep")
                    ab0, ut, mo, pq = AB0[u], UT[u], MO[u], PQ[u]
                    ua, uy = UTA[u], UTY[u]
                    a0f, b0f = ab0, ab0
                    pk = yield from acquire("f")
                    pe(lambda e: e.matmul(pk[0:C, 0:C], knT[:, h, cs], knT[:, h, cs], start=True, stop=True),
                       [*knTS], [pk], inc=False)
                    pe(lambda e: e.matmul(pk[0:C, 128:128 + C], knT[:, h, cs], qnT[:, h, cs], start=True, stop=True),
                       [*knTS, *qnTS], [pk], inc=False)
                    dve(lambda e: e.tensor_scalar(gbc[i][0:C, :], ones32[0:C, :], sm["g"][0:C, idx:idx + 1], None, ALU.mult),
                        [ones32, sm["g"]], [gbc[i]])
                    pe(lambda e: e.matmul(pk[:, 256:256 + C], gbc[i][0:C, :], U32[0:C, 0:C], start=True, stop=True),
                       [gbc[i], U32], [pk])
                    yield
                    act(lambda e: e.activation(Dm[i][0:C, 0:C], pk[0:C, 256:256 + C], AF.Relu, bias=sm["G"][0:C, idx:idx + 1],
                                               scale=-1.0), [pk, sm["G"]], [Dm[i]])
                    yield
                    act(lambda e: e.activation(Gam[i][0:C, 0:C], Dm[i][0:C, 0:C], AF.Exp, scale=-1.0), [Dm[i]], [Gam[i]])
                    act(lambda e: e.activation(eGr[i][:, 0:C], pk[:, 256:256 + C], AF.Exp), [pk], [eGr[i]])
                    yield
                    gps(lambda e: e.tensor_mul(GM[i][0:C, 0:C], Gam[i][0:C, 0:C], mUI[0:C, 0:C]), [Gam[i], mUI], [GM[i]])
                    gps(lambda e: e.tensor_mul(GMs[i][0:C, 0:C], Gam[i][0:C, 0:C], mSU[0:C, 0:C]), [Gam[i], mSU], [GMs[i]])
                    yield
                    dve(lambda e: e.scalar_tensor_tensor(ab0[0:C, 0, 0:C], pk[0:C, 0:C], sm["nbeta"][0:C, idx:idx + 1],
                                                         GMs[i][0:C, 0:C], ALU.mult, ALU.mult),
                        [pk, sm["nbeta"], GMs[i]], [a0f])
                    dve(lambda e: e.tensor_mul(QKm[u][0:C, 0:C], pk[0:C, 128:128 + C], GM[i][0:C, 0:C]), [pk, GM[i]], [QKm[u]])
                    release("f", pk)
                    yield
                    dve(lambda e: e.tensor_mul(qdT[:, h, cs], qnT[:, h, cs], eGr[i][:, 0:C]), [*qnTS, eGr[i]], [qdTS[u]])
                    release("prep", i)
                    pt = yield from acquire("b")
                    pe(lambda e: e.transpose(pt[0:C, 0:C], ab0[0:C, 0, 0:C], identb[0:C, 0:C]), [ab0, identb], [pt])
                    yield
                    act(lambda e: e.copy(ab0[0:C, 1, 0:C], pt[0:C, 0:C]), [pt], [ab0])
                    release("b", pt)
                    utv = ut[:].rearrange("p (g k) c -> p g k c", k=2)
                    gps(lambda e: e.tensor_copy(utv[0:C, :, 1, 0:C], II[0:C, :, 0:C]), [II], [uy])
                    yield
                    gps(lambda e: e.tensor_mul(utv[0:C, :, 0, 0:C], ab0[0:C, :, 0:C], bm2[0][0:C, :, 0:C]), [ab0, bm2[0]], [ua])
                    yield
                    for lev in range(4):
                        pl = yield from acquire("f")
                        plv = pl[:].rearrange("p (g k c) -> p g k c", g=2, k=2)
                        if lev < 3:
                            if C == 128:
                                pe(lambda e, pl=pl: e.matmul(pl[0:C, 0:256], ut[0:C, 2, 0:C], ut[0:C, 0:2, :].rearrange("p a c -> p (a c)"),
                                                             start=True, stop=True), [ua, uy], [pl], inc=False)
                                pe(lambda e, pl=pl: e.matmul(pl[0:C, 256:512], ut[0:C, 0, 0:C], ut[0:C, 2:4, :].rearrange("p a c -> p (a c)"),
                                                             start=True, stop=True), [ua, uy], [pl])
                            else:
                                for sl, (lh, rh) in enumerate(((2, 0), (2, 1), (0, 2), (0, 3))):
                                    pe(lambda e, pl=pl, sl=sl, lh=lh, rh=rh: e.matmul(pl[0:C, sl * 128:sl * 128 + C], ut[0:C, lh, 0:C],
                                                                                     ut[0:C, rh, 0:C], start=True, stop=True),
                                       [ua, uy], [pl], inc=(sl == 3))
                            yield
                            act(lambda e, plv=plv, pl=pl: e.copy(utv[0:C, :, 0, 0:C], plv[0:C, :, 0, 0:C]), [pl], [ua])
                            yield
                        else:
                            pe(lambda e, pl=pl: e.matmul(pl[0:C, 128:128 + C], ut[0:C, 2, 0:C], ut[0:C, 1, 0:C], start=True, stop=True),
                               [ua, uy], [pl], inc=False)
                            pe(lambda e, pl=pl: e.matmul(pl[0:C, 384:384 + C], ut[0:C, 0, 0:C], ut[0:C, 3, 0:C], start=True, stop=True),
                               [ua, uy], [pl])
                            yield
                        dve(lambda e, plv=plv, pl=pl: e.tensor_add(utv[0:C, :, 1, 0:C], plv[0:C, :, 1, 0:C], utv[0:C, :, 1, 0:C]),
                            [pl, uy], [uy])
                        release("f", pl)
                        yield
                    for l in range(1, nmerge + 1):
                        last = (l == nmerge)
                        gps(lambda e, l=l: e.tensor_mul(mo[0:C, :, 0:C], ab0[0:C, :, 0:C], bm2[l][0:C, :, 0:C]), [ab0, bm2[l]], [mo])
                        yield
                        pl = yield from acquire("f")
                        plv = pl[:].rearrange("p (g k c) -> p g k c", g=2, k=2)
                        pe(lambda e, pl=pl: e.matmul(pl[0:C, 0:C], mo[0:C, 1, 0:C], ut[0:C, 1, 0:C], start=True, stop=True),
                           [mo, uy], [pl], inc=last)
                        if not last:
                            pe(lambda e, pl=pl: e.matmul(pl[0:C, 256:256 + C], mo[0:C, 0, 0:C], ut[0:C, 3, 0:C], start=True, stop=True),
                               [mo, uy], [pl])
                        yield
                        if not last:
                            act(lambda e, plv=plv, pl=pl: e.copy(pq[0:C, :, 0:C], plv[0:C, :, 0, 0:C]), [pl], [pq])
                        else:
                            act(lambda e, pl=pl: e.copy(pq[0:C, 0, 0:C], pl[0:C, 0:C]), [pl], [pq])
                        yield
                        pe(lambda e, pl=pl: e.matmul(pl[0:C, 128:128 + C], ut[0:C, 3, 0:C], pq[0:C, 0, 0:C], start=True, stop=True),
                           [uy, pq], [pl], inc=last)
                        if not last:
                            pe(lambda e, pl=pl: e.matmul(pl[0:C, 384:384 + C], ut[0:C, 1, 0:C], pq[0:C, 1, 0:C], start=True, stop=True),
                               [uy, pq], [pl])
                        yield
                        if not last:
                            dve(lambda e, plv=plv, pl=pl: e.tensor_add(utv[0:C, :, 1, 0:C], plv[0:C, :, 1, 0:C], utv[0:C, :, 1, 0:C]),
                                [pl, uy], [uy])
                        else:
                            dve(lambda e, pl=pl: e.tensor_add(Yf[u][0:C, 0:C], pl[0:C, 128:128 + C], ut[0:C, 1, 0:C]),
                                [pl, uy], [Yf[u]])
                        release("f", pl)
                        yield

                run_tasks([unit_chain(j, h) for j in range(nch) for h in range(4)] + list(next_rms), k=UK, stagger=UST)
                stop_if(8)

                for j in range(nch):
                    cs = slice(j * C, (j + 1) * C)
                    pu = [pbank() for _ in range(4)]
                    for h in range(4):
                        pe(lambda e, h=h, cs=cs, p=pu[h]: e.matmul(p[0:C, 0:128], knT[:, h, cs], Sbf[:, h, :], start=True,
                                                                   stop=True), [*knTS, SbfH[h]], [pu[h]])
                    for h in range(4):
                        idx = j * 4 + h
                        dve(lambda e, h=h, j=j, idx=idx, p=pu[h]: e.scalar_tensor_tensor(
                            Br[h][0:C, :], p[0:C, 0:128], sm["nbe"][0:C, idx:idx + 1], vb[0:C, j, h, :], ALU.mult, ALU.add),
                            [pu[h], sm["nbe"], vb], [Br[h]])
                    for h in range(4):
                        u = j * 4 + h
                        pe(lambda e, h=h, u=u, p=pu[h]: e.matmul(p[0:C, 128:256], Yf[u][0:C, 0:C], Br[h][0:C, :], start=True,
                                                                 stop=True), [Yf[u], Br[h]], [pu[h]])
                    for h in range(4):
                        idx = j * 4 + h
                        act(lambda e, h=h, idx=idx, p=pu[h]: e.activation(vn[h][0:C, :], p[0:C, 128:256], AF.Copy,
                                                                          scale=sm["beta"][0:C, idx:idx + 1]),
                            [pu[h], sm["beta"]], [vn[h]])
                    for h in range(4):
                        u = j * 4 + h
                        pe(lambda e, h=h, cs=cs, p=pu[h]: e.matmul(p[:, 384:384 + C], Sbf[:, h, :], qdT[:, h, cs], start=True,
                                                                   stop=False), [SbfH[h], *qdTS], [pu[h]], inc=False)
                        pe(lambda e, h=h, u=u, p=pu[h]: e.matmul(p[:, 384:384 + C], vn[h][0:C, :], QKm[u][0:C, 0:C], start=False,
                                                                 stop=True), [vn[h], QKm[u]], [pu[h]], inc=False)
                        pe(lambda e, h=h, j=j, p=pu[h]: e.matmul(p[:, 256:384], ktl[0:C, j, h, :], vn[h][0:C, :], start=True,
                                                                 stop=True), [ktl, vn[h]], [pu[h]])
                    for h in range(4):
                        idx = j * 4 + h
                        dve(lambda e, h=h, idx=idx, p=pu[h]: e.scalar_tensor_tensor(
                            Sbf[:, h, :], S32[:, h, :], sm["dl"][:, idx:idx + 1], p[:, 256:384], ALU.mult, ALU.add),
                            [S32H[h], sm["dl"], pu[h]], [SbfH[h]])
                        dve(lambda e, h=h, idx=idx, p=pu[h]: e.scalar_tensor_tensor(
                            S32[:, h, :], S32[:, h, :], sm["dl"][:, idx:idx + 1], p[:, 256:384], ALU.mult, ALU.add),
                            [S32H[h], sm["dl"], pu[h]], [S32H[h]])
                        act(lambda e, h=h, cs=cs, p=pu[h]: e.copy(oT32[:, h, cs], p[:, 384:384 + C]), [pu[h]], [oTH[h]])
                stop_if(9)

                def onorm_chain(h):
                    i = yield from acquire("on")
                    act(lambda e: e.activation(osq[i][:, 0:nt], oT32[:, h, 0:nt], AF.Square), [oTH[h]], [osq[i]])
                    yield
                    pn = yield from acquire("f")
                    pe(lambda e: e.matmul(pn[:, 0:nt], onesm[:], osq[i][:, 0:nt], start=True, stop=True), [onesm, osq[i]], [pn])
                    yield
                    act(lambda e: e.activation(orn[i][:, 0:nt], pn[:, 0:nt], AF.Ln, bias=ceps[:, 0:1]), [pn, ceps], [orn[i]])
                    release("f", pn)
                    yield
                    act(lambda e: e.activation(orn[i][:, 0:nt], orn[i][:, 0:nt], AF.Exp, scale=-0.5), [orn[i]], [orn[i]])
                    yield
                    dve(lambda e: e.tensor_mul(orn[i][:, 0:nt], oT32[:, h, 0:nt], orn[i][:, 0:nt]), [oTH[h], orn[i]], [orn[i]])
                    yield
                    dve(lambda e: e.tensor_mul(mixT[:, h, 0:nt], orn[i][:, 0:nt], zs[:, h, 0:nt]), [orn[i], *zsS], [mixT])
                    release("on", i)
                    yield

                run_tasks([onorm_chain(h) for h in range(4)], k=2, stagger=2)
                stop_if(10)

                def wout_chain(j):
                    cs = slice(j * C, (j + 1) * C)
                    i = yield from acquire("xt")
                    xi = xt[i]
                    load("sp", xi, xi[0:C, :], xrows_fn(j * C, C))
                    i2 = ctr["x1t"] % 2
                    ctr["x1t"] += 1
                    xo = x1t[i2]
                    for nb in range(2):
                        pw = yield from acquire("f")
                        for kc in range(8):
                            pe(lambda e, pw=pw, kc=kc, nb=nb: e.matmul(pw[0:C, :], mixT[:, kc, cs],
                                                                      Wout[:, kc, nb * 512:(nb + 1) * 512],
                                                                      start=(kc == 0), stop=(kc == 7)),
                               [mixT, Wout], [pw], inc=(kc == 7))
                        yield
                        dve(lambda e, pw=pw, nb=nb: e.tensor_add(xo[0:C, nb * 512:(nb + 1) * 512], pw[0:C, :],
                                                                 xi[0:C, nb * 512:(nb + 1) * 512]), [pw, xi], [xo])
                        release("f", pw)
                        yield
                    release("xt", i)
                    r0 = x1row0 + j * C
                    fw.dma("pool", x1s[r0:r0 + C, :], xo[0:C, :], xo.b, reads=[xo.b], writes=[x1s_b])
                    yield

                run_tasks([wout_chain(j) for j in range(nch)])
                stop_if(11)

            def fm_to_rows_out(src, nch, nrows, dst_ap):
                for g0 in range(0, nch, 4):
                    pb = pbank()
                    for c4 in range(4):
                        pe(lambda e, c4=c4, pb=pb, g0=g0: e.transpose(pb[0:nrows, c4 * 128:(c4 + 1) * 128], src[:, g0 + c4, 0:nrows],
                                                                      ident[:]), [src, ident], [pb])
                    dve(lambda e, pb=pb: e.tensor_copy(orow[0:nrows, 0:512], pb[0:nrows, 0:512]), [pb], [orow])
                    fw.dma("pool", dst_ap[:, g0 * 128:(g0 + 4) * 128], orow[0:nrows, 0:512], orow.b, reads=[orow.b])

            def seq_finish(oc, od, op_):
                fm_to_rows_out(hist, 12, 3, oc)
                fm_to_rows_out(phist, 4, 15, op_)
                fw.dma("pool", od.rearrange("h k v -> k h v"), S32[:], S32.b, reads=[S32.b] + [x.b for x in S32H])

            descs = []
            ntp = min(NT, LP)
            for b in range(NP):
                for t in range(LP // ntp):
                    descs.append(dict(kind="p", b=b, t=t, nt=ntp, C=128, first=(t == 0), last=(t == LP // ntp - 1),
                                      xr=(lambda r0, C, b=b, t=t: xp[b, t * ntp + r0:t * ntp + r0 + C, :]),
                                      x1row0=b * LP + t * ntp))
            for b in range(NS):
                descs.append(dict(kind="s", b=b, t=0, nt=LS, C=LS, first=True, last=True,
                                  xr=(lambda r0, C, b=b: xs[b, r0:r0 + C, :]), x1row0=NP * LP + b * LS))
            for di, d in enumerate(descs):
                b = d["b"]
                if d["first"]:
                    if d["kind"] == "p":
                        dve(lambda e: e.memset(hist[:], 0.0), [], [hist])
                        dve(lambda e: e.memset(phist[:], 0.0), [], [phist])
                        dve(lambda e: e.memset(S32[:], 0.0), [], [S32] + S32H)
                        dve(lambda e: e.memset(Sbf[:], 0.0), [], [Sbf] + SbfH)
                    else:
                        rows_to_fm(sconv[b], 3, 12, hist, lambda ch: hist[:, ch, :])
                        rows_to_fm(spool[b], 15, 4, phist, lambda ch: phist[:, ch, :])
                        fw.dma("sp", S32[:], sdelta[b].rearrange("h k v -> k h v"), S32.b, writes=[S32.b] + [x.b for x in S32H])
                        dve(lambda e: e.tensor_copy(Sbf[:], S32[:]), [S32] + S32H, [Sbf] + SbfH)
                nxt = descs[di + 1] if di + 1 < len(descs) else None
                nrms = []
                if nxt:
                    nn = nxt["nt"] // nxt["C"]
                    nrms = [rms_chain(nxt["xr"], nxt["C"], j, key=di + 1) for j in range(nn)]
                    if ABPF:
                        nrms.append(ab_chain(nxt["C"], nn, smp[(di + 1) % 2], key=di + 1))
                smt(d["xr"], d["x1row0"], d["nt"], d["C"], d["first"] and d["kind"] == "p", di == 0, nrms, smp[di % 2],
                    di == 0 or not ABPF)
                if d["last"]:
                    if d["kind"] == "p":
                        seq_finish(ncp[b], ndp[b], npp[b])
                    else:
                        seq_finish(ncs[b], nds[b], nps[b])
            fw.barrier()
            stop_if(12)

        with contextlib.ExitStack() as sb_:
            Wg = mk(sb_, "Wg", [128, 8, DFF], BF16)
            Wu = mk(sb_, "Wu", [128, 8, DFF], BF16)
            Wd = mk(sb_, "Wd", [128, NFF, D], BF16)
            WgB = [T(None, "wgb") for f in range(NFF)]
            WuB = [T(None, "wub") for f in range(NFF)]
            WdB = [T(None, "wdb") for f in range(NFF)]
            n2wb = mk(sb_, "n2wb", [128, D])
            load("sp", n2wb, n2wb[:], norm2_w.partition_broadcast(128))
            pieces = []
            for f in range(NFF):
                pieces.append(("g", f))
                pieces.append(("u", f))
            for f in range(NFF):
                pieces.append(("d", f))
            pstate = {"i": 0}

            def emit_pieces(n):
                for _ in range(n):
                    i = pstate["i"]
                    if i >= len(pieces):
                        return
                    pstate["i"] = i + 1
                    kind, f = pieces[i]
                    if kind == "d":
                        fw.dma("pool", Wd[:, f, :], w_down[f * 128:(f + 1) * 128, :], WdB[f].b, writes=[WdB[f].b])
                    else:
                        wsrc = w_gate if kind == "g" else w_up
                        wdst = Wg if kind == "g" else Wu
                        dstb = WgB[f] if kind == "g" else WuB[f]
                        fw.dma("pool", wdst[:, :, f * 128:(f + 1) * 128],
                               wsrc.rearrange("(kc p) n -> p kc n", p=128)[:, :, f * 128:(f + 1) * 128], dstb.b, writes=[dstb.b])

            NTB = NT2
            NXQ = 4
            xq = [mk(sb_, "xq%d" % i, [128, D]) for i in range(NXQ)]
            hb2 = [mk(sb_, "hb2%d" % i, [128, D], BF16) for i in range(2)]
            hT2p = [mk(sb_, "hT2_%d" % i, [128, 8, NTB], BF16) for i in range(2)]
            junkt = mk(sb_, "junkt", [128, D], BF16)
            actT = mk(sb_, "actT", [128, NFF, NTB], BF16)
            sg = [mk(sb_, "sg%d" % i, [128, NTB]) for i in range(2)]
            s2 = {k: mk(sb_, "s2_" + k, [128, 8]) for k in ["ssq", "r1", "rstd", "ssq2", "r2", "rstd2"]}
            tiles = []
            r = 0
            while r < ntok:
                c = min(128, ntok - r)
                tiles.append((r, c))
                r += c
            per = NTB // 128
            xctr = 0

            def yrows(r0, c):
                if r0 < NP * LP:
                    b, o = divmod(r0, LP)
                    return [(yp[b, o:o + c, :], 0, c)]
                o = r0 - NP * LP
                res = []
                done = 0
                while done < c:
                    b, oo = divmod(o + done, LS)
                    n_ = min(c - done, LS - oo)
                    res.append((ys[b, oo:oo + n_, :], done, n_))
                    done += n_
                return res

            fstate = {}

            def ffn_front(m0):
                nonlocal xctr
                grp = tiles[m0:m0 + per]
                ntb = sum(c for _, c in grp)
                xl = []
                col = 0
                hT2 = hT2p[(m0 // per) % 2]
                fstate[m0] = (grp, ntb, xl, hT2)
                for jj, (r0, c) in enumerate(grp):
                    xi = xq[xctr % NXQ]
                    xctr += 1
                    xl.append(xi)
                    fw.dma("sp", xi[0:c, :], x1s[r0:r0 + c, :], xi.b, reads=[x1s_b], writes=[xi.b])
                    if debug:
                        fw.dma("pool", dbg1[r0:r0 + c, :], xi[0:c, :], xi.b, reads=[xi.b])
                    hbt = hb2[jj % 2]
                    act(lambda e, xi=xi, c=c, hbt=hbt, jj=jj: e.activation(hbt[0:c, :], xi[0:c, :], AF.Square,
                                                                           accum_out=s2["ssq"][0:c, jj:jj + 1]),
                        [xi], [hbt, s2["ssq"]])
                    act(lambda e, c=c, jj=jj: e.activation(s2["r1"][0:c, jj:jj + 1], s2["ssq"][0:c, jj:jj + 1], AF.Ln,
                                                           bias=ceps[0:c, 0:1], scale=1.0 / D), [s2["ssq"], ceps], [s2["r1"]])
                    act(lambda e, c=c, jj=jj: e.activation(s2["rstd"][0:c, jj:jj + 1], s2["r1"][0:c, jj:jj + 1], AF.Exp,
                                                           scale=-0.5), [s2["r1"]], [s2["rstd"]])
                    dve(lambda e, xi=xi, c=c, hbt=hbt, jj=jj: e.scalar_tensor_tensor(hbt[0:c, :], xi[0:c, :],
                                                                                    s2["rstd"][0:c, jj:jj + 1], n2wb[0:c, :],
                                                                                    ALU.mult, ALU.mult),
                        [xi, s2["rstd"], n2wb], [hbt])
                    col += c

            def ffn_front_b(m0):
                grp, ntb, xl, hT2 = fstate[m0]
                col = 0
                for jj, (r0, c) in enumerate(grp):
                    hbt = hb2[jj % 2]
                    pt = ptbank()
                    for k in range(8):
                        pe(lambda e, pt=pt, k=k, c=c, hbt=hbt: e.transpose(pt[:, k * 128:k * 128 + c],
                                                                          hbt[0:c, k * 128:(k + 1) * 128], identb[0:c, 0:c]),
                           [hbt, identb], [pt])
                    act(lambda e, pt=pt, c=c, col=col: e.copy(hT2[:, :, col:col + c],
                                                              pt[:].rearrange("p (k c) -> p k c", k=8)[:, :, 0:c]), [pt], [hT2])
                    col += c

            def ffn_mid(m0):
                grp, ntb, xl, hT2 = fstate[m0]
                for f in range(NFF):
                    pgt = pbank()
                    put = pbank()
                    for kc in range(8):
                        pe(lambda e, pgt=pgt, kc=kc, f=f: e.matmul(pgt[:, 0:ntb], Wg[:, kc, f * 128:(f + 1) * 128],
                                                                   hT2[:, kc, 0:ntb], start=(kc == 0), stop=(kc == 7)),
                           [WgB[f], hT2], [pgt], inc=(kc == 7))
                    for kc in range(8):
                        pe(lambda e, put=put, kc=kc, f=f: e.matmul(put[:, 0:ntb], Wu[:, kc, f * 128:(f + 1) * 128],
                                                                   hT2[:, kc, 0:ntb], start=(kc == 0), stop=(kc == 7)),
                           [WuB[f], hT2], [put], inc=(kc == 7))
                    sgi = sg[f % 2]
                    act(lambda e, pgt=pgt, sgi=sgi: e.activation(sgi[:, 0:ntb], pgt[:, 0:ntb], AF.Silu), [pgt], [sgi])
                    dve(lambda e, put=put, sgi=sgi, f=f: e.tensor_mul(actT[:, f, 0:ntb], sgi[:, 0:ntb], put[:, 0:ntb]),
                        [sgi, put], [actT])
                    emit_pieces(3)
                if debug and m0 == 0:
                    fw.dma("pool", dbg_h[:, :, 0:ntb], hT2[:, :, 0:ntb], hT2.b, reads=[hT2.b])
                    fw.dma("pool", dbg_a[:, :, 0:ntb], actT[:, :, 0:ntb], actT.b, reads=[actT.b])

            def ffn_back(m0):
                grp, ntb, xl, hT2 = fstate[m0]
                col = 0
                for jj, (r0, c) in enumerate(grp):
                    xi = xl[jj]
                    for nb in range(2):
                        pd = pbank()
                        for f in range(NFF):
                            pe(lambda e, pd=pd, f=f, c=c, col=col, nb=nb: e.matmul(pd[0:c, :], actT[:, f, col:col + c],
                                                                                   Wd[:, f, nb * 512:(nb + 1) * 512],
                                                                                   start=(f == 0), stop=(f == NFF - 1)),
                               [actT, WdB[f]], [pd], inc=(f == NFF - 1))
                        dve(lambda e, pd=pd, xi=xi, c=c, nb=nb: e.tensor_add(xi[0:c, nb * 512:(nb + 1) * 512], pd[0:c, :],
                                                                             xi[0:c, nb * 512:(nb + 1) * 512]), [pd, xi], [xi])
                    if debug:
                        fw.dma("pool", dbg2[r0:r0 + c, :], xi[0:c, :], xi.b, reads=[xi.b])
                    junk = junkt
                    act(lambda e, xi=xi, c=c, jj=jj, junk=junk: e.activation(junk[0:c, :], xi[0:c, :], AF.Square,
                                                                  accum_out=s2["ssq2"][0:c, jj:jj + 1]), [xi], [junk, s2["ssq2"]])
                    act(lambda e, c=c, jj=jj: e.activation(s2["r2"][0:c, jj:jj + 1], s2["ssq2"][0:c, jj:jj + 1], AF.Ln,
                                                           bias=ceps[0:c, 0:1], scale=1.0 / D), [s2["ssq2"], ceps], [s2["r2"]])
                    act(lambda e, c=c, jj=jj: e.activation(s2["rstd2"][0:c, jj:jj + 1], s2["r2"][0:c, jj:jj + 1], AF.Exp,
                                                           scale=-0.5), [s2["r2"]], [s2["rstd2"]])
                    dve(lambda e, xi=xi, c=c, jj=jj: e.scalar_tensor_tensor(xi[0:c, :], xi[0:c, :], s2["rstd2"][0:c, jj:jj + 1],
                                                                           nfw[0:c, :], ALU.mult, ALU.mult),
                        [xi, s2["rstd2"], nfw], [xi])
                    for (dst, o, n_) in yrows(r0, c):
                        fw.dma("pool", dst, xi[o:o + n_, :], xi.b, reads=[xi.b])
                    col += c
            m0s = list(range(0, len(tiles), per))
            ffn_front(m0s[0])
            ffn_front_b(m0s[0])
            emit_pieces(66)
            for mi, m0_ in enumerate(m0s):
                if mi + 1 < len(m0s):
                    ffn_front(m0s[mi + 1])
                ffn_mid(m0_)
                if mi + 1 < len(m0s):
                    ffn_front_b(m0s[mi + 1])
                ffn_back(m0_)
            fw.finish()
            fw.emit()
      except _Stop:
        pass
    return nc


_CACHE = {}


def _consts():
    idx = np.arange(128)
    ident = np.eye(128, dtype=np.float32)
    U = (idx[:, None] <= idx[None, :]).astype(np.float32)
    SU = (idx[:, None] < idx[None, :]).astype(np.float32)
    invcnt = np.zeros((128, 4, 16), np.float32)
    for g, w in enumerate((2, 4, 8, 16)):
        invcnt[:, g, :] = 1.0 / np.minimum(w, np.arange(16) + 1).astype(np.float32)
    bm = np.zeros((4, 128, 128), np.float32)
    blk = lambda n: idx // n
    bm[0] = (blk(16)[:, None] == blk(16)[None, :])
    for l in range(1, 4):
        n = 16 * 2 ** l
        bm[l] = (blk(n)[:, None] == blk(n)[None, :]) & (blk(n // 2)[:, None] != blk(n // 2)[None, :])
    return {"c_ident": ident, "c_U": U, "c_SU": SU, "c_invcnt": invcnt, "c_bmask": bm}


def run_cores(inputs, n_cores, NP, LP, NS, **kw):
    import os
    if os.environ.get('K_NT2'):
        kw['NT2'] = int(os.environ['K_NT2'])
    key = (NP, LP, NS, tuple(sorted(kw.items())))
    if key not in _CACHE:
        _CACHE[key] = build_program(NP, LP, NS, **kw)
    nc = _CACHE[key]
    f = lambda a: np.ascontiguousarray(np.asarray(a, dtype=np.float32))
    shared = {k: f(inputs[k][0]) for k in
              ["norm1_w", "w_in", "conv_w", "a_log", "dt_bias", "onorm_w", "pool_w", "pool_scale", "w_out", "norm2_w",
               "w_gate", "w_up", "w_down"]}
    shared["normf_w"] = f(inputs["normf_w"])
    shared.update(_consts())
    in_maps = []
    for i in range(n_cores):
        m = dict(shared)
        m["xp"] = f(inputs["x_prompt"][i * NP:(i + 1) * NP])
        m["xs"] = f(inputs["x_sample"][i * NS:(i + 1) * NS])
        m["sconv"] = f(inputs["state_conv"][0, i * NS:(i + 1) * NS])
        m["sdelta"] = f(inputs["state_delta"][0, i * NS:(i + 1) * NS])
        m["spool"] = f(inputs["state_pool"][0, i * NS:(i + 1) * NS])
        in_maps.append(m)
    res = run_bass_kernel_spmd(nc, in_maps, core_ids=list(range(n_cores)))
    R = res.results
    if kw.get("debug"):
        _CACHE["dbg"] = R
    cat = lambda k: np.concatenate([np.asarray(r[k], dtype=np.float32) for r in R], axis=0)
    return (cat("yp"), cat("ys"), cat("ncp")[None], cat("ndp")[None], cat("npp")[None],
            cat("ncs")[None], cat("nds")[None], cat("nps")[None])


def kernel(**inputs):
    import os
    kw = {}
    if os.environ.get('K_PST'):
        kw['PST'] = int(os.environ['K_PST'])
    if os.environ.get('K_ABPF'):
        kw['ABPF'] = int(os.environ['K_ABPF'])
    if os.environ.get('K_PK'):
        kw['PK'] = int(os.environ['K_PK'])
    if os.environ.get('K_UK'):
        kw['UK'] = int(os.environ['K_UK'])
    if os.environ.get('K_UST'):
        kw['UST'] = int(os.environ['K_UST'])
    return run_cores(inputs, 8, 2, 2048, 2, **kw)
```

```python
import contextlib
import numpy as np
import concourse.bass as bass
import concourse.mybir as mybir
from concourse.bass_utils import run_bass_kernel_spmd

F32 = mybir.dt.float32
BF16 = mybir.dt.bfloat16
AF = mybir.ActivationFunctionType
ALU = mybir.AluOpType

ENGS = ("pe", "act", "dve", "pool", "sp")
D = 1024
DFF = 2816
NFF = 22
EPS = 1e-6


class Buf:
    __slots__ = ("name", "last_write", "reads", "dsem", "dcount", "excl")

    def __init__(self, name=""):
        self.name = name
        self.excl = False
        self.last_write = None
        self.reads = []
        self.dsem = None
        self.dcount = 0


class Fw:
    def __init__(self, nc, stack):
        self.nc = nc
        self.ops = {e: [] for e in ENGS}
        self.cnt = {e: 0 for e in ENGS}
        self.sems = {}
        self.waited = {}
        self._stack = stack
        self.n_dsem = 0
        self.dma_points = {}
        self.ninst = {e: 0 for e in ENGS}
        self.dead = False
        self.maxops = None
        self.nops = 0

    def _sem(self, key):
        if key not in self.sems:
            self.sems[key] = self._stack.enter_context(self.nc.semaphore("s_%s" % (str(key).replace(" ", ""))))
        return self.sems[key]

    def _need(self, eng, deps):
        for key, val in deps.items():
            if eng == "pe" and key == "pe":
                continue
            if self.waited.get((eng, key), 0) >= val:
                continue
            self.waited[(eng, key)] = val
            sem = self._sem(key)
            self.ops[eng].append(lambda e, sem=sem, val=val: e.wait_ge(sem, val))
            self.ninst[eng] += 1

    @staticmethod
    def _add(deps, sp):
        if sp is None:
            return
        k, v = sp
        if deps.get(k, 0) < v:
            deps[k] = v

    def _deps(self, reads, writes):
        deps = {}
        for b in reads:
            self._add(deps, b.last_write)
        for b in writes:
            self._add(deps, b.last_write)
            for r in b.reads:
                self._add(deps, r)
        return deps

    def op(self, eng, fn, reads=(), writes=(), inc=True):
        self.nops += 1
        if self.dead or (self.maxops is not None and self.nops > self.maxops):
            return None
        ex = [b for b in reads if b.excl]
        if ex:
            reads = [b for b in reads if not b.excl]
            writes = list(writes) + [b for b in ex if b not in writes]
        self._need(eng, self._deps(reads, writes))
        idx = self.cnt[eng] + 1
        sp = (eng, idx)
        if inc:
            sem = self._sem(eng)
            self.ops[eng].append(lambda e, fn=fn, sem=sem: fn(e).then_inc(sem, 1))
            self.cnt[eng] = idx
        else:
            self.ops[eng].append(lambda e, fn=fn: fn(e))
        self.ninst[eng] += 1
        for b in reads:
            b.reads.append(sp)
        for b in writes:
            b.last_write = sp
            b.reads = []
        return sp

    def dma(self, q, out_ap, in_ap, sbuf_buf, reads=(), writes=(), **kw):
        self.nops += 1
        if self.dead or (self.maxops is not None and self.nops > self.maxops):
            return None
        self._need(q, self._deps(reads, writes))
        b = sbuf_buf
        if b.dsem is None:
            b.dsem = {}
        if q not in b.dsem:
            b.dsem[q] = [("d", self.n_dsem), 0]
            self.n_dsem += 1
        ent = b.dsem[q]
        sem = self._sem(ent[0])
        ent[1] += 16
        sp = (ent[0], ent[1])
        self.ops[q].append(
            lambda e, o=out_ap, i=in_ap, sem=sem, kw=kw: e.dma_start(out=o, in_=i, **kw).then_inc(sem, 16))
        self.ninst[q] += 1
        for r in reads:
            r.reads.append(sp)
        for w in writes:
            w.last_write = sp
            w.reads = []
        self.dma_points[ent[0]] = ent[1]
        return sp

    def barrier(self):
        if self.dead:
            return
        deps = {e: self.cnt[e] for e in ENGS if self.cnt[e] > 0}
        deps.update(self.dma_points)
        for e in ENGS:
            d = dict(deps)
            d.pop(e, None) if e == "pe" else None
            self._need(e, d)

    def finish(self):
        self._need("sp", dict(self.dma_points))

    def emit(self):
        ops = self.ops
        with self.nc.Block() as block:
            @block.tensor
            def _(e):
                for f in ops["pe"]:
                    f(e)

            @block.scalar
            def _(e):
                for f in ops["act"]:
                    f(e)

            @block.vector
            def _(e):
                for f in ops["dve"]:
                    f(e)

            @block.gpsimd
            def _(e):
                for f in ops["pool"]:
                    f(e)

            @block.sync
            def _(e):
                for f in ops["sp"]:
                    f(e)


class T:
    def __init__(self, t, name):
        self.t = t
        self.b = Buf(name)

    def __getitem__(self, k):
        return self.t[k]


class _Stop(Exception):
    pass


def build_program(NP, LP, NS, LS=64, NT1=256, NT2=256, debug=False, stage=99, maxops=None, PST=1, UST=1, UK=None, PK=8, ABPF=0):
    nc = bass.Bass("TRN2", target_bir_lowering=False)
    ntok = NP * LP + NS * LS

    def din(name, shape):
        return nc.dram_tensor(name, list(shape), F32, kind="ExternalInput").ap()

    def dout(name, shape):
        return nc.dram_tensor(name, list(shape), F32, kind="ExternalOutput").ap()

    xp = din("xp", [NP, LP, D])
    xs = din("xs", [NS, LS, D])
    sconv = din("sconv", [NS, 3, 1536])
    sdelta = din("sdelta", [NS, 4, 128, 128])
    spool = din("spool", [NS, 15, 512])
    norm1_w = din("norm1_w", [D])
    w_in = din("w_in", [D, 2568])
    conv_w = din("conv_w", [4, 1536])
    a_log = din("a_log", [4])
    dt_bias = din("dt_bias", [4])
    onorm_w = din("onorm_w", [128])
    pool_w = din("pool_w", [4, 128, 128])
    pool_scale = din("pool_scale", [512])
    w_out = din("w_out", [D, D])
    norm2_w = din("norm2_w", [D])
    w_gate = din("w_gate", [D, DFF])
    w_up = din("w_up", [D, DFF])
    w_down = din("w_down", [DFF, D])
    normf_w = din("normf_w", [D])
    c_ident = din("c_ident", [128, 128])
    c_U = din("c_U", [128, 128])
    c_SU = din("c_SU", [128, 128])
    c_invcnt = din("c_invcnt", [128, 4, 16])
    c_bmask = din("c_bmask", [4, 128, 128])

    yp = dout("yp", [NP, LP, D])
    ys = dout("ys", [NS, LS, D])
    ncp = dout("ncp", [NP, 3, 1536])
    ndp = dout("ndp", [NP, 4, 128, 128])
    npp = dout("npp", [NP, 15, 512])
    ncs = dout("ncs", [NS, 3, 1536])
    nds = dout("nds", [NS, 4, 128, 128])
    nps = dout("nps", [NS, 15, 512])

    x1s = nc.dram_tensor("x1s", [ntok, D], F32, kind=("ExternalOutput" if debug else "Internal")).ap()
    x1s_b = Buf("x1s")
    if debug:
        dbg1 = dout("dbg1", [ntok, D])
        dbg2 = dout("dbg2", [ntok, D])
        dbg_h = dout("dbg_h", [128, 8, NT2])
        dbg_a = dout("dbg_a", [128, NFF, NT2])

    with contextlib.ExitStack() as st:
      try:
        fw = Fw(nc, st)
        fw.maxops = maxops

        def stop_if(k):
            if stage <= k:
                if not fw.dead:
                    print('STOP stage', k, 'nops', fw.nops)
                fw.dead = True

        def mk(stack, name, shape, dt=F32):
            return T(stack.enter_context(nc.sbuf_tensor(name, list(shape), dt)), name)

        def mkp(stack, name, shape, dt=F32):
            t_ = T(stack.enter_context(nc.psum_tensor(name, list(shape), dt)), name)
            t_.b.excl = True
            return t_

        ident = mk(st, "ident", [128, 128])
        identb = mk(st, "identb", [128, 128], BF16)
        U32 = mk(st, "U32", [128, 128])
        mUI = mk(st, "mUI", [128, 128])
        mSU = mk(st, "mSU", [128, 128])
        ones32 = mk(st, "ones32", [128, 128])
        onesq = mk(st, "onesq", [128, 128], BF16)
        onesm = mk(st, "onesm", [128, 128], BF16)
        invcnt = mk(st, "invcnt", [128, 4, 16])
        cw = mk(st, "cw", [128, 12, 4])
        n1w = mk(st, "n1w", [128, 8])
        n2w = mk(st, "n2w", [128, 8])
        onw = mk(st, "onw", [128, 1])
        psc = mk(st, "psc", [128, 4])
        nfw = mk(st, "nfw", [128, D])
        dtb = mk(st, "dtb", [128, 4])
        negA = mk(st, "negA", [128, 4])
        ceps = mk(st, "ceps", [128, 4])
        rows = mk(st, "rows", [16, 512])

        PB = [mkp(st, "pb%d" % i, [128, 512]) for i in range(6)]
        PT = [mkp(st, "pt%d" % i, [128, 1024], BF16) for i in range(2)]
        pstate = {"f": 0, "b": 0}

        import collections
        free = {"f": collections.deque(PB), "b": collections.deque(PT), "raw": collections.deque([0, 1, 2, 3, 4, 5, 6, 7]),
                "pool": collections.deque([0]), "l2": collections.deque([0, 1, 2, 3]), "on": collections.deque([0, 1, 2, 3]),
                "prep": collections.deque([0, 1, 2, 3]), "xt": collections.deque([0, 1])}

        def pbank():
            p = free["f"].popleft()
            free["f"].append(p)
            return p

        def ptbank():
            p = free["b"].popleft()
            free["b"].append(p)
            return p

        def acquire(kind):
            while not free[kind]:
                yield
            return free[kind].popleft()

        def release(kind, p):
            free[kind].append(p)

        def act(fn, reads, writes):
            return fw.op("act", fn, [r.b for r in reads], [w.b for w in writes])

        def dve(fn, reads, writes):
            return fw.op("dve", fn, [r.b for r in reads], [w.b for w in writes])

        def gps(fn, reads, writes):
            return fw.op("pool", fn, [r.b for r in reads], [w.b for w in writes])

        def pe(fn, reads, writes, inc=True):
            return fw.op("pe", fn, [r.b for r in reads], [w.b for w in writes], inc=inc)

        def load(q, dst, dst_ap, src_ap, extra_reads=(), **kw):
            return fw.dma(q, dst_ap, src_ap, dst.b, reads=list(extra_reads), writes=[dst.b], **kw)

        load("sp", ident, ident[:], c_ident)
        load("sp", U32, U32[:], c_U)
        load("sp", mSU, mSU[:], c_SU)
        load("sp", invcnt, invcnt[:], c_invcnt)
        load("sp", n1w, n1w[:], norm1_w.rearrange("(k p) -> p k", p=128), allow_slow_non_contiguous=True)
        load("sp", n2w, n2w[:], norm2_w.rearrange("(k p) -> p k", p=128), allow_slow_non_contiguous=True)
        load("sp", onw, onw[:], onorm_w.rearrange("(p o) -> p o", o=1), allow_slow_non_contiguous=True)
        load("sp", psc, psc[:], pool_scale.rearrange("(g p) -> p g", p=128), allow_slow_non_contiguous=True)
        load("sp", nfw, nfw[:], normf_w.partition_broadcast(128))
        load("sp", dtb, dtb[:], dt_bias.partition_broadcast(128))
        load("sp", negA, negA[:], a_log.partition_broadcast(128))
        bm2 = [mk(st, "bm2_%d" % l, [128, 2, 128], BF16) for l in range(4)]
        II = mk(st, "II", [128, 2, 128], BF16)
        for l in range(4):
            load("sp", ones32, ones32[:], c_bmask[l])
            dve(lambda e, l=l: e.tensor_copy(bm2[l][:, 0, :], ones32[:]), [ones32], [bm2[l]])
            dve(lambda e, l=l: e.tensor_copy(bm2[l][:, 1, :], ones32[:]), [ones32], [bm2[l]])
        dve(lambda e: e.tensor_copy(identb[:], ident[:]), [ident], [identb])
        dve(lambda e: e.tensor_copy(II[:, 0, :], ident[:]), [ident], [II])
        dve(lambda e: e.tensor_copy(II[:, 1, :], ident[:]), [ident], [II])
        dve(lambda e: e.tensor_add(mUI[:], mSU[:], ident[:]), [mSU, ident], [mUI])
        dve(lambda e: e.memset(ones32[:], 1.0), [], [ones32])
        dve(lambda e: e.memset(onesq[:], 1.0), [], [onesq])
        dve(lambda e: e.memset(onesm[:], 1.0 / 128.0), [], [onesm])
        dve(lambda e: e.memset(ceps[:, 0:1], EPS), [], [ceps])
        dve(lambda e: e.memset(ceps[:, 1:2], 128.0 * EPS), [], [ceps])
        dve(lambda e: e.memset(ceps[:, 2:3], 1.0), [], [ceps])
        act(lambda e: e.activation(negA[:], negA[:], AF.Exp), [negA], [negA])
        dve(lambda e: e.tensor_scalar(negA[:], negA[:], -1.0, None, ALU.mult), [negA], [negA])

        def rows_to_fm(src_ap, nrows, nch, dst, dst_fn):
            for g0 in range(0, nch, 4):
                load("sp", rows, rows[0:nrows, 0:512], src_ap[:, g0 * 128:(g0 + 4) * 128])
                pb = pbank()
                for c4 in range(4):
                    pe(lambda e, c4=c4, pb=pb: e.transpose(pb[:, c4 * 16:c4 * 16 + nrows], rows[0:nrows, c4 * 128:(c4 + 1) * 128],
                                                           ident[0:nrows, 0:nrows]), [rows, ident], [pb])
                for c4 in range(4):
                    dve(lambda e, c4=c4, pb=pb, g0=g0: e.tensor_copy(dst_fn(g0 + c4), pb[:, c4 * 16:c4 * 16 + nrows]), [pb], [dst])

        rows_to_fm(conv_w, 4, 12, cw, lambda ch: cw[:, ch, :])
        stop_if(1)

        with contextlib.ExitStack() as sa:
            Win = mk(sa, "Win", [128, 8, 2560], BF16)
            Wab = mk(sa, "Wab", [128, 8, 8], BF16)
            Wout = mk(sa, "Wout", [128, 8, D], BF16)
            poolw = mk(sa, "poolw", [128, 4, 128], BF16)
            with contextlib.ExitStack() as s0:
                stg = [mk(s0, "stgA%d" % i, [128, 2568]) for i in range(2)]
                for kc in range(8):
                    s = stg[kc % 2]
                    load("sp", s, s[:], w_in[kc * 128:(kc + 1) * 128, :])
                    act(lambda e, s=s, kc=kc: e.activation(Win[:, kc, 0:1024], s[:, 0:1024], AF.Copy,
                                                           scale=n1w[:, kc:kc + 1]), [s, n1w], [Win])
                    dve(lambda e, s=s, kc=kc: e.tensor_scalar(Win[:, kc, 1024:2048], s[:, 1024:2048],
                                                              n1w[:, kc:kc + 1], None, ALU.mult), [s, n1w], [Win])
                    dve(lambda e, s=s, kc=kc: e.tensor_scalar(Win[:, kc, 2048:2560], s[:, 2056:2568],
                                                              n1w[:, kc:kc + 1], None, ALU.mult), [s, n1w], [Win])
                    act(lambda e, s=s, kc=kc: e.activation(Wab[:, kc, :], s[:, 2048:2056], AF.Copy,
                                                           scale=n1w[:, kc:kc + 1]), [s, n1w], [Wab])
                for kc in range(8):
                    s = stg[kc % 2]
                    load("sp", s, s[:, 0:D], w_out[kc * 128:(kc + 1) * 128, :])
                    sc = onw[:, 0:1] if kc < 4 else psc[:, kc - 4:kc - 3]
                    scb = onw if kc < 4 else psc
                    if kc % 2 == 0:
                        act(lambda e, s=s, kc=kc, sc=sc: e.activation(Wout[:, kc, :], s[:, 0:D], AF.Copy, scale=sc),
                            [s, scb], [Wout])
                    else:
                        dve(lambda e, s=s, kc=kc, sc=sc: e.tensor_scalar(Wout[:, kc, :], s[:, 0:D], sc, None, ALU.mult),
                            [s, scb], [Wout])
                s = stg[0]
                load("sp", s, s[:, 0:512].rearrange("p (g d) -> p g d", g=4), pool_w.rearrange("g c d -> c g d"))
                dve(lambda e, s=s: e.tensor_copy(poolw[:].rearrange("p g d -> p (g d)"), s[:, 0:512]), [s], [poolw])
                fw.barrier()
                stop_if(2)

            NT = NT1
            NCH = NT // 128
            NU = NCH * 4
            xt = [mk(sa, "xt%d" % i, [128, D]) for i in range(2)]
            hb = [mk(sa, "hb%d" % i, [128, D], BF16) for i in range(2)]
            hT = mk(sa, "hT", [128, 8, NT], BF16)
            raw = [mk(sa, "raw%d" % i, [128, 3 + NT]) for i in range(8)]
            rawh = [T(None, "rawh") for i in range(8)]
            acc = [mk(sa, "acc%d" % i, [128, NT]) for i in range(8)]
            qk32 = mk(sa, "qk32", [128, 8, NT])
            sqb = [mk(sa, "sqb%d" % i, [128, NT], BF16) for i in range(4)]
            rn = [mk(sa, "rn%d" % i, [128, NT]) for i in range(4)]
            qnT = mk(sa, "qnT", [128, 4, NT], BF16)
            knT = mk(sa, "knT", [128, 4, NT], BF16)
            vT = mk(sa, "vT", [128, 4, NT], BF16)
            zs = mk(sa, "zs", [128, 4, NT], BF16)
            pext = [mk(sa, "pext%d" % i, [128, 15 + NT]) for i in range(2)]
            ps2 = mk(sa, "ps2", [128, 15 + NT])
            ps4 = mk(sa, "ps4", [128, 15 + NT])
            ps8 = mk(sa, "ps8", [128, 15 + NT])
            ps16 = mk(sa, "ps16", [128, 15 + NT])
            dT = mk(sa, "dT", [128, 4, NT], BF16)
            hist = mk(sa, "hist", [128, 12, 3])
            phist = mk(sa, "phist", [128, 4, 15])
            S32 = mk(sa, "S32", [128, 4, 128])
            Sbf = mk(sa, "Sbf", [128, 4, 128], BF16)
            NS16 = NCH * 4
            sm = {k: mk(sa, "sm_" + k, [128, NS16]) for k in ["ssq", "r1", "rstd"]}
            smp = [{k: mk(sa, "sm%d_" % p_ + k, [128, NS16]) for k in
                    ["ap", "e1", "sp", "g", "e2", "beta", "nbeta", "G", "eG", "tk", "ekt", "dl", "nbe"]} for p_ in range(2)]
            rms_done = {}
            vb = mk(sa, "vb", [128, NCH, 4, 128])
            ktl = mk(sa, "ktl", [128, NCH, 4, 128], BF16)
            gbc = [mk(sa, "gbc%d" % i, [128, 128]) for i in range(4)]
            Dm = [mk(sa, "Dm%d" % i, [128, 128]) for i in range(4)]
            Gam = [mk(sa, "Gam%d" % i, [128, 128]) for i in range(4)]
            GM = [mk(sa, "GM%d" % i, [128, 128]) for i in range(4)]
            GMs = [mk(sa, "GMs%d" % i, [128, 128]) for i in range(4)]
            eGr = [mk(sa, "eGr%d" % i, [128, 128]) for i in range(4)]
            UT = [mk(sa, "UT%d" % u, [128, 4, 128], BF16) for u in range(NU)]
            AB0 = [mk(sa, "AB0%d" % u, [128, 2, 128], BF16) for u in range(NU)]
            UTA = [T(None, "uta") for u in range(NU)]
            UTY = [T(None, "uty") for u in range(NU)]
            MO = [mk(sa, "MO%d" % u, [128, 2, 128], BF16) for u in range(NU)]
            PQ = [mk(sa, "PQ%d" % u, [128, 2, 128], BF16) for u in range(NU)]
            Yf = [mk(sa, "Yf%d" % u, [128, 128], BF16) for u in range(NU)]
            QKm = [mk(sa, "QKm%d" % u, [128, 128], BF16) for u in range(NU)]
            qdT = mk(sa, "qdT", [128, 4, NT], BF16)
            Br = [mk(sa, "Br%d" % i, [128, 128], BF16) for i in range(4)]
            vn = [mk(sa, "vn%d" % i, [128, 128], BF16) for i in range(4)]
            oT32 = mk(sa, "oT32", [128, 4, NT])
            SbfH = [T(None, "sbfh") for h in range(4)]
            qk32S = [T(None, "qk32s") for _ in range(8)]
            vTS = [T(None, "vts") for _ in range(4)]
            zsS = [T(None, "zss") for _ in range(4)]
            qnTS = [T(None, "qnts") for _ in range(4)]
            knTS = [T(None, "knts") for _ in range(4)]
            qdTS = [T(None, "qdts") for _ in range(NU)]
            S32H = [T(None, "s32h") for h in range(4)]
            oTH = [T(None, "oth") for h in range(4)]
            osq = [mk(sa, "osq%d" % i, [128, NT], BF16) for i in range(4)]
            orn = [mk(sa, "orn%d" % i, [128, NT]) for i in range(4)]
            mixT = mk(sa, "mixT", [128, 8, NT], BF16)
            x1t = [mk(sa, "x1t%d" % i, [128, D]) for i in range(2)]
            orow = mk(sa, "orow", [16, 512])
            ctr = {"xt": 0, "tmp": 0, "x1t": 0}


            def rmsnorm_T(src_rows_ap, C, xtile, hbt, hT_t, col0, s_ssq, s_r1, s_rstd, sidx):
                load("sp", xtile, xtile[0:C, :], src_rows_ap)
                act(lambda e: e.activation(hbt[0:C, :], xtile[0:C, :], AF.Square,
                                           accum_out=s_ssq[0:C, sidx:sidx + 1]), [xtile], [hbt, s_ssq])
                act(lambda e: e.activation(s_r1[0:C, sidx:sidx + 1], s_ssq[0:C, sidx:sidx + 1], AF.Ln,
                                           bias=ceps[0:C, 0:1], scale=1.0 / D), [s_ssq, ceps], [s_r1])
                act(lambda e: e.activation(s_rstd[0:C, sidx:sidx + 1], s_r1[0:C, sidx:sidx + 1], AF.Exp, scale=-0.5),
                    [s_r1], [s_rstd])
                dve(lambda e: e.tensor_scalar(hbt[0:C, :], xtile[0:C, :], s_rstd[0:C, sidx:sidx + 1], None, ALU.mult),
                    [xtile, s_rstd], [hbt])
                pt = ptbank()
                for k in range(8):
                    pe(lambda e, k=k: e.transpose(pt[:, k * 128:k * 128 + C], hbt[0:C, k * 128:(k + 1) * 128],
                                                  identb[0:C, 0:C]), [hbt, identb], [pt])
                act(lambda e: e.copy(hT_t[:, :, col0:col0 + C],
                                     pt[:].rearrange("p (k c) -> p k c", k=8)[:, :, 0:C]), [pt], [hT_t])

            def run_tasks(gens, k=None, stagger=0):
                pending = list(gens)
                active = []
                rnd = 0
                last_admit = -10 ** 9
                while pending or active:
                    while pending and (k is None or len(active) < k) and (rnd - last_admit >= stagger or not active):
                        active.append(pending.pop(0))
                        last_admit = rnd
                        if stagger:
                            break
                    rnd += 1
                    for g_ in list(active):
                        try:
                            next(g_)
                        except StopIteration:
                            active.remove(g_)

            def rms_chain(xrows_fn, C, j, key=None):
                if True:
                    i = yield from acquire("xt")
                    xtile, hbt = xt[i], hb[i]
                    s_ssq, s_r1, s_rstd = sm["ssq"], sm["r1"], sm["rstd"]
                    load("sp", xtile, xtile[0:C, :], xrows_fn(j * C, C))
                    act(lambda e: e.activation(hbt[0:C, :], xtile[0:C, :], AF.Square,
                                               accum_out=s_ssq[0:C, j:j + 1]), [xtile], [hbt, s_ssq])
                    yield
                    act(lambda e: e.activation(s_r1[0:C, j:j + 1], s_ssq[0:C, j:j + 1], AF.Ln,
                                               bias=ceps[0:C, 0:1], scale=1.0 / D), [s_ssq, ceps], [s_r1])
                    yield
                    act(lambda e: e.activation(s_rstd[0:C, j:j + 1], s_r1[0:C, j:j + 1], AF.Exp, scale=-0.5),
                        [s_r1], [s_rstd])
                    yield
                    dve(lambda e: e.tensor_scalar(hbt[0:C, :], xtile[0:C, :], s_rstd[0:C, j:j + 1], None, ALU.mult),
                        [xtile, s_rstd], [hbt])
                    yield
                    pt = yield from acquire("b")
                    for k in range(8):
                        pe(lambda e, k=k: e.transpose(pt[:, k * 128:k * 128 + C], hbt[0:C, k * 128:(k + 1) * 128],
                                                      identb[0:C, 0:C]), [hbt, identb], [pt])
                    yield
                    act(lambda e: e.copy(hT[:, :, j * C:(j + 1) * C],
                                         pt[:].rearrange("p (k c) -> p k c", k=8)[:, :, 0:C]), [pt], [hT])
                    release("b", pt)
                    release("xt", i)
                    rms_done[key] = rms_done.get(key, 0) + 1
                    yield

            def ab_chain(C, nch, sm, key=None):
                n16 = nch * 4
                while key is not None and rms_done.get(key, 0) < nch:
                    yield
                pab = yield from acquire("f")
                for j in range(nch):
                    for kc in range(8):
                        pe(lambda e, j=j, kc=kc: e.matmul(pab[0:C, j * 8:(j + 1) * 8], hT[:, kc, j * C:(j + 1) * C],
                                                          Wab[:, kc, :], start=(kc == 0), stop=(kc == 7)),
                           [hT, Wab], [pab], inc=(kc == 7))
                yield
                pab3 = pab[0:C, 0:nch * 8].rearrange("p (j e) -> p j e", e=8)

                def v3(t):
                    return t[0:C, 0:n16].rearrange("p (j h) -> p j h", h=4)
                for j in range(nch):
                    dve(lambda e, j=j: e.tensor_add(sm["ap"][0:C, j * 4:(j + 1) * 4], pab[0:C, j * 8:j * 8 + 4],
                                                    dtb[0:C, :]), [pab, dtb], [sm["ap"]])
                yield
                act(lambda e: e.activation(v3(sm["e2"]), pab3[:, :, 4:8], AF.Exp, scale=-1.0), [pab], [sm["e2"]])
                release("f", pab)
                act(lambda e: e.activation(sm["e1"][0:C, 0:n16], sm["ap"][0:C, 0:n16], AF.Exp), [sm["ap"]], [sm["e1"]])
                yield
                act(lambda e: e.activation(sm["sp"][0:C, 0:n16], sm["e1"][0:C, 0:n16], AF.Ln, bias=ceps[0:C, 2:3]),
                    [sm["e1"], ceps], [sm["sp"]])
                dve(lambda e: e.tensor_scalar(sm["e2"][0:C, 0:n16], sm["e2"][0:C, 0:n16], 1.0, None, ALU.add),
                    [sm["e2"]], [sm["e2"]])
                yield
                for j in range(nch):
                    dve(lambda e, j=j: e.tensor_mul(sm["g"][0:C, j * 4:(j + 1) * 4], sm["sp"][0:C, j * 4:(j + 1) * 4],
                                                    negA[0:C, :]), [sm["sp"], negA], [sm["g"]])
                dve(lambda e: e.reciprocal(sm["beta"][0:C, 0:n16], sm["e2"][0:C, 0:n16]), [sm["e2"]], [sm["beta"]])
                yield
                dve(lambda e: e.tensor_scalar(sm["nbeta"][0:C, 0:n16], sm["beta"][0:C, 0:n16], -1.0, None, ALU.mult),
                    [sm["beta"]], [sm["nbeta"]])
                pg = yield from acquire("f")
                pe(lambda e: e.matmul(pg[0:C, 0:n16], U32[0:C, 0:C], sm["g"][0:C, 0:n16], start=True, stop=True),
                   [U32, sm["g"]], [pg], inc=False)
                pe(lambda e: e.matmul(pg[:, 64:64 + n16], ones32[0:C, :], sm["g"][0:C, 0:n16], start=True, stop=True),
                   [ones32, sm["g"]], [pg])
                yield
                dve(lambda e: e.tensor_copy(sm["G"][0:C, 0:n16], pg[0:C, 0:n16]), [pg], [sm["G"]])
                yield
                act(lambda e: e.activation(sm["eG"][0:C, 0:n16], pg[0:C, 0:n16], AF.Exp), [pg], [sm["eG"]])
                act(lambda e: e.activation(sm["dl"][:, 0:n16], pg[:, 64:64 + n16], AF.Exp), [pg], [sm["dl"]])
                yield
                dve(lambda e: e.tensor_sub(sm["tk"][0:C, 0:n16], pg[0:C, 64:64 + n16], sm["G"][0:C, 0:n16]),
                    [pg, sm["G"]], [sm["tk"]])
                release("f", pg)
                yield
                act(lambda e: e.activation(sm["ekt"][0:C, 0:n16], sm["tk"][0:C, 0:n16], AF.Exp), [sm["tk"]], [sm["ekt"]])
                dve(lambda e: e.tensor_scalar(sm["nbe"][0:C, 0:n16], sm["eG"][0:C, 0:n16], -1.0, None, ALU.mult),
                    [sm["eG"]], [sm["nbe"]])
                yield


            def smt(xrows_fn, x1row0, nt, C, first_of_prompt, do_rms, next_rms, sm, do_ab):
                nch = nt // C
                n16 = nch * 4
                nmerge = {128: 3, 64: 2}[C]
                if do_rms:
                    run_tasks([rms_chain(xrows_fn, C, j) for j in range(nch)])
                stop_if(3)

                def proj_chain(m):
                    pp = yield from acquire("f")
                    for kc in range(8):
                        pe(lambda e, kc=kc: e.matmul(pp[:, 0:nt], Win[:, kc, m * 128:(m + 1) * 128],
                                                     hT[:, kc, 0:nt], start=(kc == 0), stop=(kc == 7)),
                           [Win, hT], [pp], inc=(kc == 7))
                    yield
                    if m < 12:
                        i = yield from acquire("raw")
                        rw, ac, rwh = raw[i], acc[i], rawh[i]
                        gps(lambda e: e.tensor_copy(rw[:, 0:3], hist[:, m, :]), [hist], [rwh])
                        act(lambda e: e.copy(rw[:, 3:3 + nt], pp[:, 0:nt]), [pp], [rw])
                        yield
                        act(lambda e: e.activation(ac[:, 0:nt], pp[:, 0:nt], AF.Copy, scale=cw[:, m, 3:4]), [pp, cw], [ac])
                        release("f", pp)
                        yield
                        for jj in range(3):
                            dve(lambda e, jj=jj: e.scalar_tensor_tensor(
                                ac[:, 0:nt], rw[:, jj:jj + nt], cw[:, m, jj:jj + 1], ac[:, 0:nt], ALU.mult, ALU.add),
                                [rw, rwh, cw, ac], [ac])
                            yield
                        gps(lambda e: e.tensor_copy(hist[:, m, :], rw[:, nt:nt + 3]), [rw], [hist])
                        if m < 8:
                            act(lambda e: e.activation(qk32[:, m, 0:nt], ac[:, 0:nt], AF.Silu), [ac], [qk32S[m]])
                        else:
                            act(lambda e: e.activation(vT[:, m - 8, 0:nt], ac[:, 0:nt], AF.Silu), [ac], [vTS[m - 8]])
                        release("raw", i)
                        yield
                    elif m < 16:
                        act(lambda e: e.activation(zs[:, m - 12, 0:nt], pp[:, 0:nt], AF.Silu), [pp], [zsS[m - 12]])
                        release("f", pp)
                        yield
                    else:
                        g = m - 16
                        _slot = yield from acquire("pool")
                        px = pext[0]
                        L = 15 + nt
                        dve(lambda e: e.tensor_copy(px[:, 0:15], phist[:, g, :]), [phist], [px])
                        act(lambda e: e.copy(px[:, 15:15 + nt], pp[:, 0:nt]), [pp], [px])
                        release("f", pp)
                        yield
                        gps(lambda e: e.tensor_add(ps2[:, 1:L], px[:, 1:L], px[:, 0:L - 1]), [px], [ps2])
                        yield
                        cur = ps2
                        if g >= 1:
                            gps(lambda e: e.tensor_add(ps4[:, 3:L], ps2[:, 3:L], ps2[:, 1:L - 2]), [ps2], [ps4])
                            cur = ps4
                            yield
                        if g >= 2:
                            gps(lambda e: e.tensor_add(ps8[:, 7:L], ps4[:, 7:L], ps4[:, 3:L - 4]), [ps4], [ps8])
                            cur = ps8
                            yield
                        if g >= 3:
                            gps(lambda e: e.tensor_add(ps16[:, 15:L], ps8[:, 15:L], ps8[:, 7:L - 8]), [ps8], [ps16])
                            cur = ps16
                            yield
                        win = 2 ** (g + 1)
                        dve(lambda e: e.scalar_tensor_tensor(dT[:, g, 0:nt], cur[:, 15:L], 1.0 / win, px[:, 15:L],
                                                             ALU.mult, ALU.subtract), [cur, px], [dT])
                        if first_of_prompt:
                            dve(lambda e: e.tensor_mul(cur[:, 15:31], cur[:, 15:31], invcnt[:, g, :]), [cur, invcnt], [cur])
                            dve(lambda e: e.tensor_sub(dT[:, g, 0:16], cur[:, 15:31], px[:, 15:31]), [cur, px], [dT])
                        dve(lambda e: e.tensor_copy(phist[:, g, :], px[:, nt:nt + 15]), [px], [phist])
                        release("pool", 0)
                        yield
                        pq = yield from acquire("f")
                        pe(lambda e: e.matmul(pq[:, 0:nt], poolw[:, g, :], dT[:, g, 0:nt], start=True, stop=True),
                           [poolw, dT], [pq])
                        yield
                        act(lambda e: e.copy(mixT[:, 4 + g, 0:nt], pq[:, 0:nt]), [pq], [mixT])
                        release("f", pq)
                        yield

                run_tasks([proj_chain(m) for m in range(20)], k=PK, stagger=PST)

                def l2_chain(m):
                    i = yield from acquire("l2")
                    sq, rr = sqb[i], rn[i]
                    gps(lambda e: e.tensor_mul(sq[:, 0:nt], qk32[:, m, 0:nt], qk32[:, m, 0:nt]), [qk32S[m]], [sq])
                    yield
                    pn = yield from acquire("f")
                    pe(lambda e: e.matmul(pn[:, 0:nt], onesq[:], sq[:, 0:nt], start=True, stop=True), [onesq, sq], [pn])
                    yield
                    if m < 4:
                        act(lambda e: e.activation(rr[:, 0:nt], pn[:, 0:nt], AF.Ln, bias=ceps[:, 1:2], scale=128.0),
                            [pn, ceps], [rr])
                    else:
                        act(lambda e: e.activation(rr[:, 0:nt], pn[:, 0:nt], AF.Ln, bias=ceps[:, 0:1], scale=1.0),
                            [pn, ceps], [rr])
                    release("f", pn)
                    yield
                    act(lambda e: e.activation(rr[:, 0:nt], rr[:, 0:nt], AF.Exp, scale=-0.5), [rr], [rr])
                    yield
                    dstT = qnT if m < 4 else knT
                    dve(lambda e: e.tensor_mul(dstT[:, m % 4, 0:nt], qk32[:, m, 0:nt], rr[:, 0:nt]), [qk32S[m], rr], [(qnTS if m < 4 else knTS)[m % 4]])
                    release("l2", i)
                    yield

                run_tasks(([ab_chain(C, nch, sm)] if do_ab else []) + [l2_chain(m) for m in range(8)], k=5, stagger=1)
                stop_if(5)

                def tok_chain(j):
                    pt = yield from acquire("b")
                    for h in range(4):
                        pe(lambda e, h=h: e.transpose(pt[0:C, h * 128:(h + 1) * 128], vT[:, h, j * C:(j + 1) * C],
                                                      identb[:]), [*vTS, identb], [pt])
                    for h in range(4):
                        pe(lambda e, h=h: e.transpose(pt[0:C, 512 + h * 128:512 + (h + 1) * 128],
                                                      knT[:, h, j * C:(j + 1) * C], identb[:]), [*knTS, identb], [pt])
                    yield
                    dve(lambda e: e.tensor_copy(vb[0:C, j, :, :].rearrange("p h d -> p (h d)"), pt[0:C, 0:512]), [pt], [vb])
                    yield
                    for h in range(4):
                        idx = j * 4 + h
                        act(lambda e, h=h, idx=idx: e.activation(
                            ktl[0:C, j, h, :], pt[0:C, 512 + h * 128:512 + (h + 1) * 128], AF.Copy,
                            scale=sm["ekt"][0:C, idx:idx + 1]), [pt, sm["ekt"]], [ktl])
                    release("b", pt)
                    yield

                run_tasks([tok_chain(j) for j in range(nch)])
                stop_if(6)

                def unit_chain(j, h):
                    u = j * 4 + h
                    idx = u
                    cs = slice(j * C, (j + 1) * C)
                    i = yield from acquire("prep")
                    ab0, ut, mo, pq = AB0[u], UT[u], MO[u], PQ[u]
                    ua, uy = UTA[u], UTY[u]
                    a0f, b0f = ab0, ab0
                    pk = yield from acquire("f")
                    pe(lambda e: e.matmul(pk[0:C, 0:C], knT[:, h, cs], knT[:, h, cs], start=True, stop=True),
                       [*knTS], [pk], inc=False)
                    pe(lambda e: e.matmul(pk[0:C, 128:128 + C], knT[:, h, cs], qnT[:, h, cs], start=True, stop=True),
                       [*knTS, *qnTS], [pk], inc=False)
                    dve(lambda e: e.tensor_scalar(gbc[i][0:C, :], ones32[0:C, :], sm["g"][0:C, idx:idx + 1], None, ALU.mult),
                        [ones32, sm["g"]], [gbc[i]])
                    pe(lambda e: e.matmul(pk[:, 256:256 + C], gbc[i][0:C, :], U32[0:C, 0:C], start=True, stop=True),
                       [gbc[i], U32], [pk])
                    yield
                    act(lambda e: e.activation(Dm[i][0:C, 0:C], pk[0:C, 256:256 + C], AF.Relu, bias=sm["G"][0:C, idx:idx + 1],
                                               scale=-1.0), [pk, sm["G"]], [Dm[i]])
                    yield
                    act(lambda e: e.activation(Gam[i][0:C, 0:C], Dm[i][0:C, 0:C], AF.Exp, scale=-1.0), [Dm[i]], [Gam[i]])
                    act(lambda e: e.activation(eGr[i][:, 0:C], pk[:, 256:256 + C], AF.Exp), [pk], [eGr[i]])
                    yield
                    gps(lambda e: e.tensor_mul(GM[i][0:C, 0:C], Gam[i][0:C, 0:C], mUI[0:C, 0:C]), [Gam[i], mUI], [GM[i]])
                    gps(lambda e: e.tensor_mul(GMs[i][0:C, 0:C], Gam[i][0:C, 0:C], mSU[0:C, 0:C]), [Gam[i], mSU], [GMs[i]])
                    yield
                    dve(lambda e: e.scalar_tensor_tensor(ab0[0:C, 0, 0:C], pk[0:C, 0:C], sm["nbeta"][0:C, idx:idx + 1],
                                                         GMs[i][0:C, 0:C], ALU.mult, ALU.mult),
                        [pk, sm["nbeta"], GMs[i]], [a0f])
                    dve(lambda e: e.tensor_mul(QKm[u][0:C, 0:C], pk[0:C, 128:128 + C], GM[i][0:C, 0:C]), [pk, GM[i]], [QKm[u]])
                    release("f", pk)
                    yield
                    dve(lambda e: e.tensor_mul(qdT[:, h, cs], qnT[:, h, cs], eGr[i][:, 0:C]), [*qnTS, eGr[i]], [qdTS[u]])
                    release("prep", i)
                    pt = yield from acquire("b")
                    pe(lambda e: e.transpose(pt[0:C, 0:C], ab0[0:C, 0, 0:C], identb[0:C, 0:C]), [ab0, identb], [pt])
                    yield
                    act(lambda e: e.copy(ab0[0:C, 1, 0:C], pt[0:C, 0:C]), [pt], [ab0])
                    release("b", pt)
                    utv = ut[:].rearrange("p (g k) c -> p g k c", k=2)
                    gps(lambda e: e.tensor_copy(utv[0:C, :, 1, 0:C], II[0:C, :, 0:C]), [II], [uy])
                    yield
                    gps(lambda e: e.tensor_mul(utv[0:C, :, 0, 0:C], ab0[0:C, :, 0:C], bm2[0][0:C, :, 0:C]), [ab0, bm2[0]], [ua])
                    yield
                    for lev in range(4):
                        pl = yield from acquire("f")
                        plv = pl[:].rearrange("p (g k c) -> p g k c", g=2, k=2)
                        if lev < 3:
                            if C == 128:
                                pe(lambda e, pl=pl: e.matmul(pl[0:C, 0:256], ut[0:C, 2, 0:C], ut[0:C, 0:2, :].rearrange("p a c -> p (a c)"),
                                                             start=True, stop=True), [ua, uy], [pl], inc=False)
                                pe(lambda e, pl=pl: e.matmul(pl[0:C, 256:512], ut[0:C, 0, 0:C], ut[0:C, 2:4, :].rearrange("p a c -> p (a c)"),
                                                             start=True, stop=True), [ua, uy], [pl])
                            else:
                                for sl, (lh, rh) in enumerate(((2, 0), (2, 1), (0, 2), (0, 3))):
                                    pe(lambda e, pl=pl, sl=sl, lh=lh, rh=rh: e.matmul(pl[0:C, sl * 128:sl * 128 + C], ut[0:C, lh, 0:C],
                                                                                     ut[0:C, rh, 0:C], start=True, stop=True),
                                       [ua, uy], [pl], inc=(sl == 3))
                            yield
                            act(lambda e, plv=plv, pl=pl: e.copy(utv[0:C, :, 0, 0:C], plv[0:C, :, 0, 0:C]), [pl], [ua])
                            yield
                        else:
                            pe(lambda e, pl=pl: e.matmul(pl[0:C, 128:128 + C], ut[0:C, 2, 0:C], ut[0:C, 1, 0:C], start=True, stop=True),
                               [ua, uy], [pl], inc=False)
                            pe(lambda e, pl=pl: e.matmul(pl[0:C, 384:384 + C], ut[0:C, 0, 0:C], ut[0:C, 3, 0:C], start=True, stop=True),
                               [ua, uy], [pl])
                            yield
                        dve(lambda e, plv=plv, pl=pl: e.tensor_add(utv[0:C, :, 1, 0:C], plv[0:C, :, 1, 0:C], utv[0:C, :, 1, 0:C]),
                            [pl, uy], [uy])
                        release("f", pl)
                        yield
                    for l in range(1, nmerge + 1):
                        last = (l == nmerge)
                        gps(lambda e, l=l: e.tensor_mul(mo[0:C, :, 0:C], ab0[0:C, :, 0:C], bm2[l][0:C, :, 0:C]), [ab0, bm2[l]], [mo])
                        yield
                        pl = yield from acquire("f")
                        plv = pl[:].rearrange("p (g k c) -> p g k c", g=2, k=2)
                        pe(lambda e, pl=pl: e.matmul(pl[0:C, 0:C], mo[0:C, 1, 0:C], ut[0:C, 1, 0:C], start=True, stop=True),
                           [mo, uy], [pl], inc=last)
                        if not last:
                            pe(lambda e, pl=pl: e.matmul(pl[0:C, 256:256 + C], mo[0:C, 0, 0:C], ut[0:C, 3, 0:C], start=True, stop=True),
                               [mo, uy], [pl])
                        yield
                        if not last:
                            act(lambda e, plv=plv, pl=pl: e.copy(pq[0:C, :, 0:C], plv[0:C, :, 0, 0:C]), [pl], [pq])
                        else:
                            act(lambda e, pl=pl: e.copy(pq[0:C, 0, 0:C], pl[0:C, 0:C]), [pl], [pq])
                        yield
                        pe(lambda e, pl=pl: e.matmul(pl[0:C, 128:128 + C], ut[0:C, 3, 0:C], pq[0:C, 0, 0:C], start=True, stop=True),
                           [uy, pq], [pl], inc=last)
                        if not last:
                            pe(lambda e, pl=pl: e.matmul(pl[0:C, 384:384 + C], ut[0:C, 1, 0:C], pq[0:C, 1, 0:C], start=True, stop=True),
                               [uy, pq], [pl])
                        yield
                        if not last:
                            dve(lambda e, plv=plv, pl=pl: e.tensor_add(utv[0:C, :, 1, 0:C], plv[0:C, :, 1, 0:C], utv[0:C, :, 1, 0:C]),
                                [pl, uy], [uy])
                        else:
                            dve(lambda e, pl=pl: e.tensor_add(Yf[u][0:C, 0:C], pl[0:C, 128:128 + C], ut[0:C, 1, 0:C]),
                                [pl, uy], [Yf[u]])
                        release("f", pl)
                        yield

                run_tasks([unit_chain(j, h) for j in range(nch) for h in range(4)] + list(next_rms), k=UK, stagger=UST)
                stop_if(8)

                for j in range(nch):
                    cs = slice(j * C, (j + 1) * C)
                    pu = [pbank() for _ in range(4)]
                    for h in range(4):
                        pe(lambda e, h=h, cs=cs, p=pu[h]: e.matmul(p[0:C, 0:128], knT[:, h, cs], Sbf[:, h, :], start=True,
                                                                   stop=True), [*knTS, SbfH[h]], [pu[h]])
                    for h in range(4):
                        idx = j * 4 + h
                        dve(lambda e, h=h, j=j, idx=idx, p=pu[h]: e.scalar_tensor_tensor(
                            Br[h][0:C, :], p[0:C, 0:128], sm["nbe"][0:C, idx:idx + 1], vb[0:C, j, h, :], ALU.mult, ALU.add),
                            [pu[h], sm["nbe"], vb], [Br[h]])
                    for h in range(4):
                        u = j * 4 + h
                        pe(lambda e, h=h, u=u, p=pu[h]: e.matmul(p[0:C, 128:256], Yf[u][0:C, 0:C], Br[h][0:C, :], start=True,
                                                                 stop=True), [Yf[u], Br[h]], [pu[h]])
                    for h in range(4):
                        idx = j * 4 + h
                        act(lambda e, h=h, idx=idx, p=pu[h]: e.activation(vn[h][0:C, :], p[0:C, 128:256], AF.Copy,
                                                                          scale=sm["beta"][0:C, idx:idx + 1]),
                            [pu[h], sm["beta"]], [vn[h]])
                    for h in range(4):
                        u = j * 4 + h
                        pe(lambda e, h=h, cs=cs, p=pu[h]: e.matmul(p[:, 384:384 + C], Sbf[:, h, :], qdT[:, h, cs], start=True,
                                                                   stop=False), [SbfH[h], *qdTS], [pu[h]], inc=False)
                        pe(lambda e, h=h, u=u, p=pu[h]: e.matmul(p[:, 384:384 + C], vn[h][0:C, :], QKm[u][0:C, 0:C], start=False,
                                                                 stop=True), [vn[h], QKm[u]], [pu[h]], inc=False)
                        pe(lambda e, h=h, j=j, p=pu[h]: e.matmul(p[:, 256:384], ktl[0:C, j, h, :], vn[h][0:C, :], start=True,
                                                                 stop=True), [ktl, vn[h]], [pu[h]])
                    for h in range(4):
                        idx = j * 4 + h
                        dve(lambda e, h=h, idx=idx, p=pu[h]: e.scalar_tensor_tensor(
                            Sbf[:, h, :], S32[:, h, :], sm["dl"][:, idx:idx + 1], p[:, 256:384], ALU.mult, ALU.add),
                            [S32H[h], sm["dl"], pu[h]], [SbfH[h]])
                        dve(lambda e, h=h, idx=idx, p=pu[h]: e.scalar_tensor_tensor(
                            S32[:, h, :], S32[:, h, :], sm["dl"][:, idx:idx + 1], p[:, 256:384], ALU.mult, ALU.add),
                            [S32H[h], sm["dl"], pu[h]], [S32H[h]])
                        act(lambda e, h=h, cs=cs, p=pu[h]: e.copy(oT32[:, h, cs], p[:, 384:384 + C]), [pu[h]], [oTH[h]])
                stop_if(9)

                def onorm_chain(h):
                    i = yield from acquire("on")
                    act(lambda e: e.activation(osq[i][:, 0:nt], oT32[:, h, 0:nt], AF.Square), [oTH[h]], [osq[i]])
                    yield
                    pn = yield from acquire("f")
                    pe(lambda e: e.matmul(pn[:, 0:nt], onesm[:], osq[i][:, 0:nt], start=True, stop=True), [onesm, osq[i]], [pn])
                    yield
                    act(lambda e: e.activation(orn[i][:, 0:nt], pn[:, 0:nt], AF.Ln, bias=ceps[:, 0:1]), [pn, ceps], [orn[i]])
                    release("f", pn)
                    yield
                    act(lambda e: e.activation(orn[i][:, 0:nt], orn[i][:, 0:nt], AF.Exp, scale=-0.5), [orn[i]], [orn[i]])
                    yield
                    dve(lambda e: e.tensor_mul(orn[i][:, 0:nt], oT32[:, h, 0:nt], orn[i][:, 0:nt]), [oTH[h], orn[i]], [orn[i]])
                    yield
                    dve(lambda e: e.tensor_mul(mixT[:, h, 0:nt], orn[i][:, 0:nt], zs[:, h, 0:nt]), [orn[i], *zsS], [mixT])
                    release("on", i)
                    yield

                run_tasks([onorm_chain(h) for h in range(4)], k=2, stagger=2)
                stop_if(10)

                def wout_chain(j):
                    cs = slice(j * C, (j + 1) * C)
                    i = yield from acquire("xt")
                    xi = xt[i]
                    load("sp", xi, xi[0:C, :], xrows_fn(j * C, C))
                    i2 = ctr["x1t"] % 2
                    ctr["x1t"] += 1
                    xo = x1t[i2]
                    for nb in range(2):
                        pw = yield from acquire("f")
                        for kc in range(8):
                            pe(lambda e, pw=pw, kc=kc, nb=nb: e.matmul(pw[0:C, :], mixT[:, kc, cs],
                                                                      Wout[:, kc, nb * 512:(nb + 1) * 512],
                                                                      start=(kc == 0), stop=(kc == 7)),
                               [mixT, Wout], [pw], inc=(kc == 7))
                        yield
                        dve(lambda e, pw=pw, nb=nb: e.tensor_add(xo[0:C, nb * 512:(nb + 1) * 512], pw[0:C, :],
                                                                 xi[0:C, nb * 512:(nb + 1) * 512]), [pw, xi], [xo])
                        release("f", pw)
                        yield
                    release("xt", i)
                    r0 = x1row0 + j * C
                    fw.dma("pool", x1s[r0:r0 + C, :], xo[0:C, :], xo.b, reads=[xo.b], writes=[x1s_b])
                    yield

                run_tasks([wout_chain(j) for j in range(nch)])
                stop_if(11)

            def fm_to_rows_out(src, nch, nrows, dst_ap):
                for g0 in range(0, nch, 4):
                    pb = pbank()
                    for c4 in range(4):
                        pe(lambda e, c4=c4, pb=pb, g0=g0: e.transpose(pb[0:nrows, c4 * 128:(c4 + 1) * 128], src[:, g0 + c4, 0:nrows],
                                                                      ident[:]), [src, ident], [pb])
                    dve(lambda e, pb=pb: e.tensor_copy(orow[0:nrows, 0:512], pb[0:nrows, 0:512]), [pb], [orow])
                    fw.dma("pool", dst_ap[:, g0 * 128:(g0 + 4) * 128], orow[0:nrows, 0:512], orow.b, reads=[orow.b])

            def seq_finish(oc, od, op_):
                fm_to_rows_out(hist, 12, 3, oc)
                fm_to_rows_out(phist, 4, 15, op_)
                fw.dma("pool", od.rearrange("h k v -> k h v"), S32[:], S32.b, reads=[S32.b] + [x.b for x in S32H])

            descs = []
            ntp = min(NT, LP)
            for b in range(NP):
                for t in range(LP // ntp):
                    descs.append(dict(kind="p", b=b, t=t, nt=ntp, C=128, first=(t == 0), last=(t == LP // ntp - 1),
                                      xr=(lambda r0, C, b=b, t=t: xp[b, t * ntp + r0:t * ntp + r0 + C, :]),
                                      x1row0=b * LP + t * ntp))
            for b in range(NS):
                descs.append(dict(kind="s", b=b, t=0, nt=LS, C=LS, first=True, last=True,
                                  xr=(lambda r0, C, b=b: xs[b, r0:r0 + C, :]), x1row0=NP * LP + b * LS))
            for di, d in enumerate(descs):
                b = d["b"]
                if d["first"]:
                    if d["kind"] == "p":
                        dve(lambda e: e.memset(hist[:], 0.0), [], [hist])
                        dve(lambda e: e.memset(phist[:], 0.0), [], [phist])
                        dve(lambda e: e.memset(S32[:], 0.0), [], [S32] + S32H)
                        dve(lambda e: e.memset(Sbf[:], 0.0), [], [Sbf] + SbfH)
                    else:
                        rows_to_fm(sconv[b], 3, 12, hist, lambda ch: hist[:, ch, :])
                        rows_to_fm(spool[b], 15, 4, phist, lambda ch: phist[:, ch, :])
                        fw.dma("sp", S32[:], sdelta[b].rearrange("h k v -> k h v"), S32.b, writes=[S32.b] + [x.b for x in S32H])
                        dve(lambda e: e.tensor_copy(Sbf[:], S32[:]), [S32] + S32H, [Sbf] + SbfH)
                nxt = descs[di + 1] if di + 1 < len(descs) else None
                nrms = []
                if nxt:
                    nn = nxt["nt"] // nxt["C"]
                    nrms = [rms_chain(nxt["xr"], nxt["C"], j, key=di + 1) for j in range(nn)]
                    if ABPF:
                        nrms.append(ab_chain(nxt["C"], nn, smp[(di + 1) % 2], key=di + 1))
                smt(d["xr"], d["x1row0"], d["nt"], d["C"], d["first"] and d["kind"] == "p", di == 0, nrms, smp[di % 2],
                    di == 0 or not ABPF)
                if d["last"]:
                    if d["kind"] == "p":
                        seq_finish(ncp[b], ndp[b], npp[b])
                    else:
                        seq_finish(ncs[b], nds[b], nps[b])
            fw.barrier()
            stop_if(12)

        with contextlib.ExitStack() as sb_:
            Wg = mk(sb_, "Wg", [128, 8, DFF], BF16)
            Wu = mk(sb_, "Wu", [128, 8, DFF], BF16)
            Wd = mk(sb_, "Wd", [128, NFF, D], BF16)
            WgB = [T(None, "wgb") for f in range(NFF)]
            WuB = [T(None, "wub") for f in range(NFF)]
            WdB = [T(None, "wdb") for f in range(NFF)]
            n2wb = mk(sb_, "n2wb", [128, D])
            load("sp", n2wb, n2wb[:], norm2_w.partition_broadcast(128))
            pieces = []
            for f in range(NFF):
                pieces.append(("g", f))
                pieces.append(("u", f))
            for f in range(NFF):
                pieces.append(("d", f))
            pstate = {"i": 0}

            def emit_pieces(n):
                for _ in range(n):
                    i = pstate["i"]
                    if i >= len(pieces):
                        return
                    pstate["i"] = i + 1
                    kind, f = pieces[i]
                    if kind == "d":
                        fw.dma("pool", Wd[:, f, :], w_down[f * 128:(f + 1) * 128, :], WdB[f].b, writes=[WdB[f].b])
                    else:
                        wsrc = w_gate if kind == "g" else w_up
                        wdst = Wg if kind == "g" else Wu
                        dstb = WgB[f] if kind == "g" else WuB[f]
                        fw.dma("pool", wdst[:, :, f * 128:(f + 1) * 128],
                               wsrc.rearrange("(kc p) n -> p kc n", p=128)[:, :, f * 128:(f + 1) * 128], dstb.b, writes=[dstb.b])

            NTB = NT2
            NXQ = 4
            xq = [mk(sb_, "xq%d" % i, [128, D]) for i in range(NXQ)]
            hb2 = [mk(sb_, "hb2%d" % i, [128, D], BF16) for i in range(2)]
            hT2p = [mk(sb_, "hT2_%d" % i, [128, 8, NTB], BF16) for i in range(2)]
            junkt = mk(sb_, "junkt", [128, D], BF16)
            actT = mk(sb_, "actT", [128, NFF, NTB], BF16)
            sg = [mk(sb_, "sg%d" % i, [128, NTB]) for i in range(2)]
            s2 = {k: mk(sb_, "s2_" + k, [128, 8]) for k in ["ssq", "r1", "rstd", "ssq2", "r2", "rstd2"]}
            tiles = []
            r = 0
            while r < ntok:
                c = min(128, ntok - r)
                tiles.append((r, c))
                r += c
            per = NTB // 128
            xctr = 0

            def yrows(r0, c):
                if r0 < NP * LP:
                    b, o = divmod(r0, LP)
                    return [(yp[b, o:o + c, :], 0, c)]
                o = r0 - NP * LP
                res = []
                done = 0
                while done < c:
                    b, oo = divmod(o + done, LS)
                    n_ = min(c - done, LS - oo)
                    res.append((ys[b, oo:oo + n_, :], done, n_))
                    done += n_
                return res

            fstate = {}

            def ffn_front(m0):
                nonlocal xctr
                grp = tiles[m0:m0 + per]
                ntb = sum(c for _, c in grp)
                xl = []
                col = 0
                hT2 = hT2p[(m0 // per) % 2]
                fstate[m0] = (grp, ntb, xl, hT2)
                for jj, (r0, c) in enumerate(grp):
                    xi = xq[xctr % NXQ]
                    xctr += 1
                    xl.append(xi)
                    fw.dma("sp", xi[0:c, :], x1s[r0:r0 + c, :], xi.b, reads=[x1s_b], writes=[xi.b])
                    if debug:
                        fw.dma("pool", dbg1[r0:r0 + c, :], xi[0:c, :], xi.b, reads=[xi.b])
                    hbt = hb2[jj % 2]
                    act(lambda e, xi=xi, c=c, hbt=hbt, jj=jj: e.activation(hbt[0:c, :], xi[0:c, :], AF.Square,
                                                                           accum_out=s2["ssq"][0:c, jj:jj + 1]),
                        [xi], [hbt, s2["ssq"]])
                    act(lambda e, c=c, jj=jj: e.activation(s2["r1"][0:c, jj:jj + 1], s2["ssq"][0:c, jj:jj + 1], AF.Ln,
                                                           bias=ceps[0:c, 0:1], scale=1.0 / D), [s2["ssq"], ceps], [s2["r1"]])
                    act(lambda e, c=c, jj=jj: e.activation(s2["rstd"][0:c, jj:jj + 1], s2["r1"][0:c, jj:jj + 1], AF.Exp,
                                                           scale=-0.5), [s2["r1"]], [s2["rstd"]])
                    dve(lambda e, xi=xi, c=c, hbt=hbt, jj=jj: e.scalar_tensor_tensor(hbt[0:c, :], xi[0:c, :],
                                                                                    s2["rstd"][0:c, jj:jj + 1], n2wb[0:c, :],
                                                                                    ALU.mult, ALU.mult),
                        [xi, s2["rstd"], n2wb], [hbt])
                    col += c

            def ffn_front_b(m0):
                grp, ntb, xl, hT2 = fstate[m0]
                col = 0
                for jj, (r0, c) in enumerate(grp):
                    hbt = hb2[jj % 2]
                    pt = ptbank()
                    for k in range(8):
                        pe(lambda e, pt=pt, k=k, c=c, hbt=hbt: e.transpose(pt[:, k * 128:k * 128 + c],
                                                                          hbt[0:c, k * 128:(k + 1) * 128], identb[0:c, 0:c]),
                           [hbt, identb], [pt])
                    act(lambda e, pt=pt, c=c, col=col: e.copy(hT2[:, :, col:col + c],
                                                              pt[:].rearrange("p (k c) -> p k c", k=8)[:, :, 0:c]), [pt], [hT2])
                    col += c

            def ffn_mid(m0, nxt=None):
                grp, ntb, xl, hT2 = fstate[m0]
                for f in range(NFF):
                    pgt = pbank()
                    put = pbank()
                    for kc in range(8):
                        pe(lambda e, pgt=pgt, kc=kc, f=f: e.matmul(pgt[:, 0:ntb], Wg[:, kc, f * 128:(f + 1) * 128],
                                                                   hT2[:, kc, 0:ntb], start=(kc == 0), stop=(kc == 7)),
                           [WgB[f], hT2], [pgt], inc=(kc == 7))
                    for kc in range(8):
                        pe(lambda e, put=put, kc=kc, f=f: e.matmul(put[:, 0:ntb], Wu[:, kc, f * 128:(f + 1) * 128],
                                                                   hT2[:, kc, 0:ntb], start=(kc == 0), stop=(kc == 7)),
                           [WuB[f], hT2], [put], inc=(kc == 7))
                    sgi = sg[f % 2]
                    act(lambda e, pgt=pgt, sgi=sgi: e.activation(sgi[:, 0:ntb], pgt[:, 0:ntb], AF.Silu), [pgt], [sgi])
                    dve(lambda e, put=put, sgi=sgi, f=f: e.tensor_mul(actT[:, f, 0:ntb], sgi[:, 0:ntb], put[:, 0:ntb]),
                        [sgi, put], [actT])
                    emit_pieces(3)
                    if f == 8 and nxt is not None:
                        ffn_front(nxt)
                if debug and m0 == 0:
                    fw.dma("pool", dbg_h[:, :, 0:ntb], hT2[:, :, 0:ntb], hT2.b, reads=[hT2.b])
                    fw.dma("pool", dbg_a[:, :, 0:ntb], actT[:, :, 0:ntb], actT.b, reads=[actT.b])

            def ffn_back(m0):
                grp, ntb, xl, hT2 = fstate[m0]
                col = 0
                for jj, (r0, c) in enumerate(grp):
                    xi = xl[jj]
                    for nb in range(2):
                        pd = pbank()
                        for f in range(NFF):
                            pe(lambda e, pd=pd, f=f, c=c, col=col, nb=nb: e.matmul(pd[0:c, :], actT[:, f, col:col + c],
                                                                                   Wd[:, f, nb * 512:(nb + 1) * 512],
                                                                                   start=(f == 0), stop=(f == NFF - 1)),
                               [actT, WdB[f]], [pd], inc=(f == NFF - 1))
                        dve(lambda e, pd=pd, xi=xi, c=c, nb=nb: e.tensor_add(xi[0:c, nb * 512:(nb + 1) * 512], pd[0:c, :],
                                                                             xi[0:c, nb * 512:(nb + 1) * 512]), [pd, xi], [xi])
                    if debug:
                        fw.dma("pool", dbg2[r0:r0 + c, :], xi[0:c, :], xi.b, reads=[xi.b])
                    junk = junkt
                    act(lambda e, xi=xi, c=c, jj=jj, junk=junk: e.activation(junk[0:c, :], xi[0:c, :], AF.Square,
                                                                  accum_out=s2["ssq2"][0:c, jj:jj + 1]), [xi], [junk, s2["ssq2"]])
                    act(lambda e, c=c, jj=jj: e.activation(s2["r2"][0:c, jj:jj + 1], s2["ssq2"][0:c, jj:jj + 1], AF.Ln,
                                                           bias=ceps[0:c, 0:1], scale=1.0 / D), [s2["ssq2"], ceps], [s2["r2"]])
                    act(lambda e, c=c, jj=jj: e.activation(s2["rstd2"][0:c, jj:jj + 1], s2["r2"][0:c, jj:jj + 1], AF.Exp,
                                                           scale=-0.5), [s2["r2"]], [s2["rstd2"]])
                    dve(lambda e, xi=xi, c=c, jj=jj: e.scalar_tensor_tensor(xi[0:c, :], xi[0:c, :], s2["rstd2"][0:c, jj:jj + 1],
                                                                           nfw[0:c, :], ALU.mult, ALU.mult),
                        [xi, s2["rstd2"], nfw], [xi])
                    for (dst, o, n_) in yrows(r0, c):
                        fw.dma("pool", dst, xi[o:o + n_, :], xi.b, reads=[xi.b])
                    col += c
            m0s = list(range(0, len(tiles), per))
            ffn_front(m0s[0])
            ffn_front_b(m0s[0])
            emit_pieces(66)
            for mi, m0_ in enumerate(m0s):
                ffn_mid(m0_, m0s[mi + 1] if mi + 1 < len(m0s) else None)
                if mi + 1 < len(m0s):
                    ffn_front_b(m0s[mi + 1])
                ffn_back(m0_)
            fw.finish()
            fw.emit()
      except _Stop:
        pass
    return nc


_CACHE = {}


def _consts():
    idx = np.arange(128)
    ident = np.eye(128, dtype=np.float32)
    U = (idx[:, None] <= idx[None, :]).astype(np.float32)
    SU = (idx[:, None] < idx[None, :]).astype(np.float32)
    invcnt = np.zeros((128, 4, 16), np.float32)
    for g, w in enumerate((2, 4, 8, 16)):
        invcnt[:, g, :] = 1.0 / np.minimum(w, np.arange(16) + 1).astype(np.float32)
    bm = np.zeros((4, 128, 128), np.float32)
    blk = lambda n: idx // n
    bm[0] = (blk(16)[:, None] == blk(16)[None, :])
    for l in range(1, 4):
        n = 16 * 2 ** l
        bm[l] = (blk(n)[:, None] == blk(n)[None, :]) & (blk(n // 2)[:, None] != blk(n // 2)[None, :])
    return {"c_ident": ident, "c_U": U, "c_SU": SU, "c_invcnt": invcnt, "c_bmask": bm}


def run_cores(inputs, n_cores, NP, LP, NS, **kw):
    import os
    if os.environ.get('K_NT2'):
        kw['NT2'] = int(os.environ['K_NT2'])
    key = (NP, LP, NS, tuple(sorted(kw.items())))
    if key not in _CACHE:
        _CACHE[key] = build_program(NP, LP, NS, **kw)
    nc = _CACHE[key]
    f = lambda a: np.ascontiguousarray(np.asarray(a, dtype=np.float32))
    shared = {k: f(inputs[k][0]) for k in
              ["norm1_w", "w_in", "conv_w", "a_log", "dt_bias", "onorm_w", "pool_w", "pool_scale", "w_out", "norm2_w",
               "w_gate", "w_up", "w_down"]}
    shared["normf_w"] = f(inputs["normf_w"])
    shared.update(_consts())
    in_maps = []
    for i in range(n_cores):
        m = dict(shared)
        m["xp"] = f(inputs["x_prompt"][i * NP:(i + 1) * NP])
        m["xs"] = f(inputs["x_sample"][i * NS:(i + 1) * NS])
        m["sconv"] = f(inputs["state_conv"][0, i * NS:(i + 1) * NS])
        m["sdelta"] = f(inputs["state_delta"][0, i * NS:(i + 1) * NS])
        m["spool"] = f(inputs["state_pool"][0, i * NS:(i + 1) * NS])
        in_maps.append(m)
    res = run_bass_kernel_spmd(nc, in_maps, core_ids=list(range(n_cores)))
    R = res.results
    if kw.get("debug"):
        _CACHE["dbg"] = R
    cat = lambda k: np.concatenate([np.asarray(r[k], dtype=np.float32) for r in R], axis=0)
    return (cat("yp"), cat("ys"), cat("ncp")[None], cat("ndp")[None], cat("npp")[None],
            cat("ncs")[None], cat("nds")[None], cat("nps")[None])


def kernel(**inputs):
    import os
    kw = {}
    if os.environ.get('K_PST'):
        kw['PST'] = int(os.environ['K_PST'])
    if os.environ.get('K_ABPF'):
        kw['ABPF'] = int(os.environ['K_ABPF'])
    if os.environ.get('K_PK'):
        kw['PK'] = int(os.environ['K_PK'])
    if os.environ.get('K_UK'):
        kw['UK'] = int(os.environ['K_UK'])
    if os.environ.get('K_UST'):
        kw['UST'] = int(os.environ['K_UST'])
    return run_cores(inputs, 8, 2, 2048, 2, **kw)
```
